# Optimizing a Trainium2 kernel written in Bass

```python
import math
import jax, jax.numpy as jnp
from jax import lax
import numpy as np

D_MODEL = 2048
BATCH = 8
SEQ = 2048
DEPTH = 2

CHUNK = 128
POOL_WIDTH = D_MODEL
POOL_GROUPS = 4
POOL_WINDOWS = (2, 4, 8, 16)
POOL_GDIM = POOL_WIDTH // POOL_GROUPS
SSM_WIDTH = D_MODEL
SSM_HEAD_DIM = 64
SSM_HEADS = SSM_WIDTH // SSM_HEAD_DIM
SSM_GROUPS = 4
SSM_STATE = 128
SSM_CONV = 4
SSM_BC = SSM_GROUPS * SSM_STATE
SSM_CONV_DIM = SSM_WIDTH + 2 * SSM_BC
MLSTM_WIDTH = D_MODEL
MLSTM_HEADS = 8
MLSTM_HEAD_DIM = MLSTM_WIDTH // MLSTM_HEADS
N_BRANCH = 3
ALPHA = (2 * DEPTH) ** 0.25
BETA = (8 * DEPTH) ** -0.25
EPS = 1e-5

IN_SIZES = (POOL_WIDTH, POOL_WIDTH,
            SSM_WIDTH, SSM_WIDTH, SSM_BC, SSM_BC, SSM_HEADS,
            MLSTM_WIDTH, MLSTM_WIDTH, MLSTM_WIDTH, MLSTM_WIDTH, MLSTM_WIDTH,
            MLSTM_HEADS, MLSTM_HEADS,
            N_BRANCH * D_MODEL)
IN_DIM = sum(IN_SIZES)

kernel_name = "hybrid_pool_ssd_mlstm_gated_deepnorm"

F32 = jnp.float32


def _split_points():
    pts, acc = [], 0
    for n in IN_SIZES[:-1]:
        acc += n
        pts.append(acc)
    return pts


def layer_norm(h, g, b):
    h = h.astype(F32)
    mu = jnp.mean(h, -1, keepdims=True)
    var = jnp.mean(jnp.square(h - mu), -1, keepdims=True)
    return (h - mu) * lax.rsqrt(var + EPS) * g + b


def pool_mixer(u, w_pool, pool_scale):
    b, s, _ = u.shape
    uf = u.astype(F32).reshape(b, s, POOL_GROUPS, POOL_GDIM)
    cs = jnp.cumsum(uf, axis=1)
    cs = jnp.concatenate([jnp.zeros_like(cs[:, :1]), cs], axis=1)
    t = jnp.arange(s)
    outs = []
    for g, w in enumerate(POOL_WINDOWS):
        start = jnp.maximum(t + 1 - w, 0)
        win_sum = cs[:, 1:, g] - cs[:, start, g]
        cnt = jnp.minimum(t + 1, w).astype(F32)
        outs.append(win_sum / cnt[None, :, None] - uf[:, :, g])
    pooled = jnp.stack(outs, axis=2)
    mixed = jnp.einsum('bsgc,gcd->bsgd', pooled, w_pool.astype(F32))
    return mixed.reshape(b, s, POOL_WIDTH) * pool_scale.astype(F32)


def causal_depthwise_conv(x, w, bias):
    k = w.shape[0]
    y = lax.conv_general_dilated(x, w[:, None, :], window_strides=(1,), padding=[(k - 1, 0)],
                                 dimension_numbers=('NWC', 'WIO', 'NWC'),
                                 feature_group_count=x.shape[-1])
    return y + bias


def segsum(a):
    T = a.shape[-1]
    cs = jnp.cumsum(a, axis=-1)
    diff = cs[..., :, None] - cs[..., None, :]
    mask = jnp.tril(jnp.ones((T, T), dtype=bool))
    return jnp.where(mask, diff, -jnp.inf)


def ssd_scan(xh, a, bm, cm):
    b, s, h, p = xh.shape
    g, n = bm.shape[2], bm.shape[3]
    r = h // g
    c = s // CHUNK
    X = xh.reshape(b, c, CHUNK, g, r, p)
    A = a.reshape(b, c, CHUNK, g, r).transpose(0, 3, 4, 1, 2)
    Bc = bm.reshape(b, c, CHUNK, g, n)
    Cc = cm.reshape(b, c, CHUNK, g, n)
    A_cs = jnp.cumsum(A, axis=-1)
    Lmat = jnp.exp(segsum(A))
    CB = jnp.einsum('bclgn,bcsgn->bcgls', Cc, Bc)
    y_diag = jnp.einsum('bcgls,bgrcls,bcsgrp->bclgrp', CB, Lmat, X)
    decay_states = jnp.exp(A_cs[..., -1:] - A_cs)
    states = jnp.einsum('bclgn,bgrcl,bclgrp->bcgrpn', Bc, decay_states, X)
    chunk_decay = jnp.exp(A_cs[..., -1])

    def step(carry, inp):
        st, dec = inp
        return carry * dec[..., None, None] + st, carry

    init = jnp.zeros((b, g, r, p, n), F32)
    _, prev = lax.scan(step, init, (states.transpose(1, 0, 2, 3, 4, 5),
                                    chunk_decay.transpose(3, 0, 1, 2)))
    prev = prev.transpose(1, 0, 2, 3, 4, 5)
    y_off = jnp.einsum('bclgn,bcgrpn,bgrcl->bclgrp', Cc, prev, jnp.exp(A_cs))
    return (y_diag + y_off).reshape(b, s, h * p)


def mamba2_branch(xs, z, bm, cm, dt_raw, conv_w, conv_b, dt_bias, a_log, d_skip, norm_w):
    b, s, _ = xs.shape
    xbc = jnp.concatenate([xs, bm, cm], axis=-1).astype(F32)
    xbc = jax.nn.silu(causal_depthwise_conv(xbc, conv_w.astype(F32), conv_b.astype(F32)))
    xs, bm, cm = jnp.split(xbc, [SSM_WIDTH, SSM_WIDTH + SSM_BC], axis=-1)
    dt = jax.nn.softplus(dt_raw.astype(F32) + dt_bias.astype(F32))
    A = -jnp.exp(a_log.astype(F32))
    xh = xs.reshape(b, s, SSM_HEADS, SSM_HEAD_DIM)
    y = ssd_scan(xh * dt[..., None], dt * A,
                 bm.reshape(b, s, SSM_GROUPS, SSM_STATE), cm.reshape(b, s, SSM_GROUPS, SSM_STATE))
    y = y + (xh * d_skip.astype(F32)[:, None]).reshape(b, s, SSM_WIDTH)
    y = y * jax.nn.silu(z.astype(F32))
    yg = y.reshape(b, s, SSM_GROUPS, -1)
    yg = yg * lax.rsqrt(jnp.mean(yg * yg, -1, keepdims=True) + EPS)
    return yg.reshape(b, s, SSM_WIDTH) * norm_w.astype(F32)


def mlstm_chunkwise(q, k, v, ig, lf):
    b, h, s, d = q.shape
    c = s // CHUNK
    q = q.reshape(b, h, c, CHUNK, d)
    k = k.reshape(b, h, c, CHUNK, d)
    v = v.reshape(b, h, c, CHUNK, d)
    ig = ig.reshape(b, h, c, CHUNK)
    lf = lf.reshape(b, h, c, CHUNK)
    bcum = jnp.cumsum(lf, axis=-1)
    b_last = bcum[..., -1]
    w_log = b_last[..., None] - bcum + ig
    m_loc = jnp.max(w_log, axis=-1)
    wgt = jnp.exp(w_log - m_loc[..., None])
    C_loc = jnp.einsum('bhcl,bhcld,bhcle->bhcde', wgt, k, v)
    n_loc = jnp.einsum('bhcl,bhcld->bhcd', wgt, k)

    def step(carry, inp):
        Cp, npv, mp = carry
        Cl, nl, ml, bl = inp
        m_new = jnp.maximum(bl + mp, ml)
        s_old = jnp.exp(bl + mp - m_new)
        s_loc = jnp.exp(ml - m_new)
        Cn = Cp * s_old[..., None, None] + Cl * s_loc[..., None, None]
        nn = npv * s_old[..., None] + nl * s_loc[..., None]
        return (Cn, nn, m_new), (Cp, npv, mp)

    init = (jnp.zeros((b, h, d, d), F32), jnp.zeros((b, h, d), F32), jnp.zeros((b, h), F32))
    _, (C_prev, n_prev, m_prev) = lax.scan(
        step, init, (C_loc.transpose(2, 0, 1, 3, 4), n_loc.transpose(2, 0, 1, 3),
                     m_loc.transpose(2, 0, 1), b_last.transpose(2, 0, 1)))
    C_prev = C_prev.transpose(1, 2, 0, 3, 4)
    n_prev = n_prev.transpose(1, 2, 0, 3)
    m_prev = m_prev.transpose(1, 2, 0)
    Dlog = bcum[..., :, None] - bcum[..., None, :] + ig[..., None, :]
    mask = jnp.tril(jnp.ones((CHUNK, CHUNK), dtype=bool))
    Dlog = jnp.where(mask, Dlog, -jnp.inf)
    inter_log = bcum + m_prev[..., None]
    m_t = jnp.maximum(jnp.max(Dlog, axis=-1), inter_log)
    Dw = jnp.exp(Dlog - m_t[..., None])
    inter_w = jnp.exp(inter_log - m_t)
    qk = jnp.einsum('bhcld,bhcsd->bhcls', q, k) * Dw
    num = (jnp.einsum('bhcls,bhcse->bhcle', qk, v)
           + inter_w[..., None] * jnp.einsum('bhcld,bhcde->bhcle', q, C_prev))
    den = jnp.sum(qk, axis=-1) + inter_w * jnp.einsum('bhcld,bhcd->bhcl', q, n_prev)
    hcell = num / jnp.maximum(jnp.abs(den), jnp.exp(-m_t))[..., None]
    return hcell.reshape(b, h, s, d)


def mlstm_branch(q, k, v, o, i_pre, f_pre, z, i_bias, f_bias, norm_w):
    b, s, _ = q.shape

    def heads(t):
        return t.astype(F32).reshape(b, s, MLSTM_HEADS, MLSTM_HEAD_DIM).transpose(0, 2, 1, 3)

    qh, kh, vh = heads(q), heads(k) * (MLSTM_HEAD_DIM ** -0.5), heads(v)
    ig = (i_pre.astype(F32) + i_bias.astype(F32)).transpose(0, 2, 1)
    lf = jax.nn.log_sigmoid(f_pre.astype(F32) + f_bias.astype(F32)).transpose(0, 2, 1)
    hc = mlstm_chunkwise(qh, kh, vh, ig, lf).transpose(0, 2, 1, 3)
    hc = jax.nn.sigmoid(o.astype(F32)).reshape(b, s, MLSTM_HEADS, MLSTM_HEAD_DIM) * hc
    mu = jnp.mean(hc, -1, keepdims=True)
    var = jnp.mean(jnp.square(hc - mu), -1, keepdims=True)
    hc = (hc - mu) * lax.rsqrt(var + EPS)
    hc = hc.reshape(b, s, MLSTM_WIDTH) * norm_w.astype(F32)
    return hc * jax.nn.silu(z.astype(F32))


def hybrid_layer(x, w_in, b_gate, w_pool, pool_scale, conv_w, conv_b, dt_bias, a_log, d_skip,
                 ssm_norm_w, i_bias, f_bias, mlstm_norm_w, w_branch, w_out, ln_g, ln_b):
    b, s, d = x.shape
    proj = jnp.einsum('bsd,de->bse', x, w_in)
    (pu, pz, sx, sz, sB, sC, sdt, mq, mk, mv, mo, mz, mi, mf, gates) = jnp.split(
        proj, _split_points(), axis=-1)
    y_pool = pool_mixer(pu, w_pool, pool_scale) * jax.nn.silu(pz.astype(F32))
    y_ssm = mamba2_branch(sx, sz, sB, sC, sdt, conv_w, conv_b, dt_bias, a_log, d_skip, ssm_norm_w)
    y_mlstm = mlstm_branch(mq, mk, mv, mo, mi, mf, mz, i_bias, f_bias, mlstm_norm_w)
    g = jax.nn.sigmoid(gates.astype(F32).reshape(b, s, N_BRANCH, d) + b_gate.astype(F32))
    wb = w_branch.astype(F32)
    merged = (g[:, :, 0] * jnp.einsum('bsw,wd->bsd', y_pool, wb[0])
              + g[:, :, 1] * jnp.einsum('bsw,wd->bsd', y_ssm, wb[1])
              + g[:, :, 2] * jnp.einsum('bsw,wd->bsd', y_mlstm, wb[2]))
    out = jnp.einsum('bsd,de->bse', merged, w_out.astype(F32))
    h = ALPHA * x.astype(F32) + out
    return layer_norm(h, ln_g.astype(F32), ln_b.astype(F32)).astype(x.dtype)


def setup_inputs(seed: int = 0) -> dict:
    key = jax.random.key(seed)
    ks = jax.random.split(key, 20)
    L = DEPTH
    nrm = jax.random.normal
    x = nrm(ks[0], (BATCH, SEQ, D_MODEL), F32)
    w_in = nrm(ks[1], (L, D_MODEL, IN_DIM), F32) * D_MODEL ** -0.5
    b_gate = 0.01 * nrm(ks[2], (L, N_BRANCH, D_MODEL), F32)
    w_pool = nrm(ks[3], (L, POOL_GROUPS, POOL_GDIM, POOL_GDIM), F32) * POOL_GDIM ** -0.5
    pool_scale = 1.0 + 0.02 * nrm(ks[4], (L, POOL_WIDTH), F32)
    conv_w = nrm(ks[5], (L, SSM_CONV, SSM_CONV_DIM), F32) * SSM_CONV ** -0.5
    conv_b = 0.01 * nrm(ks[6], (L, SSM_CONV_DIM), F32)
    dt0 = jnp.exp(jax.random.uniform(ks[7], (L, SSM_HEADS), F32, math.log(1e-3), math.log(1e-1)))
    dt_bias = dt0 + jnp.log(-jnp.expm1(-dt0))
    a_log = jnp.log(jax.random.uniform(ks[8], (L, SSM_HEADS), F32, 1.0, 16.0))
    d_skip = 1.0 + 0.01 * nrm(ks[9], (L, SSM_HEADS), F32)
    ssm_norm_w = 1.0 + 0.02 * nrm(ks[10], (L, SSM_WIDTH), F32)
    i_bias = 0.1 * nrm(ks[11], (L, MLSTM_HEADS), F32)
    f_bias = jnp.linspace(3.0, 6.0, MLSTM_HEADS, dtype=F32)[None, :] + 0.01 * nrm(ks[12], (L, MLSTM_HEADS), F32)
    mlstm_norm_w = 1.0 + 0.02 * nrm(ks[13], (L, MLSTM_WIDTH), F32)
    w_branch = nrm(ks[14], (L, N_BRANCH, D_MODEL, D_MODEL), F32) * (D_MODEL ** -0.5) * BETA
    w_out = nrm(ks[15], (L, D_MODEL, D_MODEL), F32) * (D_MODEL ** -0.5) * BETA
    ln_g = 1.0 + 0.02 * nrm(ks[16], (L, D_MODEL), F32)
    ln_b = 0.01 * nrm(ks[17], (L, D_MODEL), F32)
    return {"x": x, "w_in": w_in, "b_gate": b_gate, "w_pool": w_pool, "pool_scale": pool_scale,
            "conv_w": conv_w, "conv_b": conv_b, "dt_bias": dt_bias, "a_log": a_log,
            "d_skip": d_skip, "ssm_norm_w": ssm_norm_w, "i_bias": i_bias, "f_bias": f_bias,
            "mlstm_norm_w": mlstm_norm_w, "w_branch": w_branch, "w_out": w_out,
            "ln_g": ln_g, "ln_b": ln_b}


def reference(x, w_in, b_gate, w_pool, pool_scale, conv_w, conv_b, dt_bias, a_log, d_skip,
              ssm_norm_w, i_bias, f_bias, mlstm_norm_w, w_branch, w_out, ln_g, ln_b):
    h = x
    for l in range(DEPTH):
        h = hybrid_layer(h, w_in[l], b_gate[l], w_pool[l], pool_scale[l], conv_w[l], conv_b[l],
                         dt_bias[l], a_log[l], d_skip[l], ssm_norm_w[l], i_bias[l], f_bias[l],
                         mlstm_norm_w[l], w_branch[l], w_out[l], ln_g[l], ln_b[l])
    return h
```

```python
import contextlib
import numpy as np
import concourse.bass as bass
import concourse.mybir as mybir
from concourse.bass_utils import run_bass_kernel_spmd

F32 = mybir.dt.float32
BF16 = mybir.dt.bfloat16
AF = mybir.ActivationFunctionType
ALU = mybir.AluOpType

D = 2048
SEQ = 2048
NB = 16
IN_DIM = 25648
EPS = 1e-5
DEPTH = 2
ALPHA = (2 * DEPTH) ** 0.25
SEM_LIMIT = 30000
NSLOT = 4

SEC = {"pu": 0, "pz": 2048, "sx": 4096, "sz": 6144, "sB": 8192, "sC": 8704, "mq": 9248, "mk": 11296,
       "mv": 13344, "mo": 15392, "mz": 17440, "g": 19504}
SECN = {"pu": 8, "pz": 8, "sx": 8, "sz": 8, "sB": 2, "sC": 2, "mq": 8, "mk": 8, "mv": 8, "mo": 8, "mz": 8, "g": 24}
BLK0 = {}
_acc = 0
for _k in SEC:
    BLK0[_k] = _acc
    _acc += SECN[_k]
NWBLK = _acc


class Sem:
    __slots__ = ("h", "count", "owner")

    def __init__(self, h, owner):
        self.h = h
        self.count = 0
        self.owner = owner


class Buf:
    __slots__ = ("ap", "w", "r", "excl", "name")

    def __init__(self, ap, name="", excl=False):
        self.ap = ap
        self.w = None
        self.r = {}
        self.excl = excl
        self.name = name

    def __getitem__(self, k):
        return self.ap[k]


import sys as _sys


def _caller_line():
    f = _sys._getframe(2)
    while f is not None and f.f_code.co_name in ("emit", "pe", "act", "dve", "dma", "<lambda>", "proj_fm", "proj_tm", "to_fm", "rstd_from", "sp_load", "load_w"):
        f = f.f_back
    return f.f_lineno if f is not None else 0


class _Rec:
    def __getattr__(self, name):
        def f(*a, **k):
            return (name, a, k)
        return f


REC = _Rec()


class Prog:
    STREAMS = ("pe", "act", "dve", "pool", "sp")

    def __init__(self, nc, stack):
        self.nc = nc
        self.stack = stack
        self.ops = {s: [] for s in self.STREAMS}
        self.known = {s: {} for s in self.STREAMS}
        self.nsem = 0
        self.engsem = {s: self.new_sem(s) for s in ("pe", "act", "dve", "pool")}
        self.pending = {s: False for s in self.STREAMS}
        self.marks = []
        self.tags = {s: [] for s in self.STREAMS}
        self.names = {s: [] for s in self.STREAMS}

    def new_sem(self, owner):
        h = self.stack.enter_context(self.nc.semaphore(f"s{self.nsem}_{owner}"))
        self.nsem += 1
        return Sem(h, owner)

    def sbuf(self, shape, dtype, name):
        return self.stack.enter_context(self.nc.sbuf_tensor(name, list(shape), dtype))

    def psum(self, shape, dtype, name):
        return self.stack.enter_context(self.nc.psum_tensor(name, list(shape), dtype))

    def emit(self, stream, fn, outs=(), ins=(), inc=True, dma_sem=None):
        waits = {}
        known = self.known[stream]

        def need(sem, val):
            if stream == "pe" and sem.owner == "pe":
                return
            if known.get(sem, 0) >= val:
                return
            if waits.get(sem, 0) < val:
                waits[sem] = val

        for b in ins:
            if b.w is not None:
                need(*b.w)
            if b.excl:
                for sem, val in b.r.items():
                    need(sem, val)
        for b in outs:
            if b.w is not None:
                need(*b.w)
            for sem, val in b.r.items():
                need(sem, val)
        for sem, val in waits.items():
            known[sem] = val
        if fn is None:
            self.ops[stream].append((list(waits.items()), None, None, 0))
            return
        if dma_sem is not None:
            dma_sem.count += 16
            ev = (dma_sem, dma_sem.count)
            amt = 16
        else:
            es = self.engsem[stream]
            if es.count >= SEM_LIMIT and not self.pending[stream]:
                es = self.new_sem(stream)
                self.engsem[stream] = es
            if inc:
                es.count += 1
                ev = (es, es.count)
                self.pending[stream] = False
            else:
                ev = (es, es.count + 1)
                self.pending[stream] = True
            amt = 1 if inc else 0
        for b in ins:
            if b.excl:
                b.w = ev
                b.r = {}
            elif b.r.get(ev[0], 0) < ev[1]:
                b.r[ev[0]] = ev[1]
        for b in outs:
            b.w = ev
            b.r = {}
        self.ops[stream].append((list(waits.items()), fn(REC), ev[0], amt))
        self.tags[stream].append(_caller_line())

    def pe(self, fn, outs, ins, inc=True):
        self.emit("pe", fn, outs, ins, inc)

    def act(self, fn, outs, ins):
        self.emit("act", fn, outs, ins)

    def dve(self, fn, outs, ins):
        self.emit("dve", fn, outs, ins)

    def dma(self, stream, out_ap, in_ap, outs, ins, sem):
        self.emit(stream, lambda e: e.dma_start(out=out_ap, in_=in_ap), outs, ins, dma_sem=sem)

    def mark(self, name):
        self.marks.append((name, {t: sum(1 for o in self.ops[t] if o[1] is not None) for t in self.STREAMS}))

    def barrier(self, streams=("pe", "act", "dve"), extra=()):
        evs = [(self.engsem[t], self.engsem[t].count) for t in streams] + list(extra)
        for t in streams:
            assert not self.pending[t]
            waits = []
            for sem, val in evs:
                if val == 0 or (t == "pe" and sem.owner == "pe"):
                    continue
                if self.known[t].get(sem, 0) >= val:
                    continue
                self.known[t][sem] = val
                waits.append((sem, val))
            self.ops[t].append((waits, None, None, 0))

    def wait_for(self, stream, bufs):
        self.emit(stream, None, outs=bufs, ins=())

    def finish(self):
        nc = self.nc
        ops = self.ops
        with nc.Block() as block:
            def run(stream, e):
                for waits, fn, sem, amt in ops[stream]:
                    for wsem, val in waits:
                        e.wait_ge(wsem.h, val)
                    if fn is None:
                        continue
                    ins = getattr(e, fn[0])(*fn[1], **fn[2])
                    self.names[stream].append(ins.ins.name)
                    if amt:
                        ins.then_inc(sem.h, amt)

            @block.tensor
            def _(e):
                run("pe", e)

            @block.scalar
            def _(e):
                run("act", e)

            @block.vector
            def _(e):
                run("dve", e)

            @block.gpsimd
            def _(e):
                run("pool", e)

            @block.sync
            def _(e):
                run("sp", e)


def bc(ap, axis, shape):
    return ap.unsqueeze(axis).to_broadcast(list(shape))


def build_nc(layers, T, seq=SEQ):
    NT = T // 128
    NTILES = seq // T
    NL = len(layers)
    nc = bass.Bass("TRN2", target_bir_lowering=False)
    x_d = nc.dram_tensor("x", [seq, D], F32, kind="ExternalInput").ap()
    win_d = nc.dram_tensor("w_in_r", [DEPTH * NWBLK, 128, 4096], F32, kind="ExternalInput").ap()
    ws_d = nc.dram_tensor("ws_r", [DEPTH, 128, 768], F32, kind="ExternalInput").ap()
    wb_d = nc.dram_tensor("wb_r", [DEPTH * 24, 128, 4096], F32, kind="ExternalInput").ap()
    wo_d = nc.dram_tensor("wo_r", [DEPTH * 8, 128, 4096], F32, kind="ExternalInput").ap()
    wp_d = nc.dram_tensor("wp_r", [DEPTH * 4, 128, 2048], F32, kind="ExternalInput").ap()
    cols_d = nc.dram_tensor("cols_l", [DEPTH, 128, 184], F32, kind="ExternalInput").ap()
    srow_d = nc.dram_tensor("srow_l", [DEPTH, 128, 112], F32, kind="ExternalInput").ap()
    rows_d = nc.dram_tensor("rows_l", [DEPTH * 4, 128, 2048], F32, kind="ExternalInput").ap()
    cst_d = nc.dram_tensor("consts", [128, 704], F32, kind="ExternalInput").ap()
    out_d = nc.dram_tensor("out", [seq, D], F32, kind="ExternalOutput").ap()
    sc_d = {"in": nc.dram_tensor("wsc_in", [DEPTH * NWBLK, 128, 4096], BF16).ap(),
            "b": nc.dram_tensor("wsc_b", [DEPTH * 24, 128, 4096], BF16).ap(),
            "o": nc.dram_tensor("wsc_o", [DEPTH * 8, 128, 4096], BF16).ap(),
            "p": nc.dram_tensor("wsc_p", [DEPTH * 4, 128, 2048], BF16).ap()}
    src_d = {"in": win_d, "b": wb_d, "o": wo_d, "p": wp_d}

    with contextlib.ExitStack() as st:
        P = Prog(nc, st)

        def SB(shape, dt, name):
            return Buf(P.sbuf(shape, dt, name)[:], name)

        cst = SB([128, 704], F32, "cst")
        identb = SB([128, 128], BF16, "identb")
        cols = [SB([128, 184], F32, f"cols{i}") for i in range(NL)]
        srow = [SB([128, 112], F32, f"srow{i}") for i in range(NL)]
        Sst = [SB([128, 2048], F32, f"Sst{i}") for i in range(NL)]
        Cst = [[SB([128, 2, 257], F32, f"Cst{i}_{h}") for h in range(8)] for i in range(NL)]
        ctail = [SB([128, 24, 3], F32, f"ctail{i}") for i in range(NL)]
        ptail = [SB([128, 16, 16], F32, f"ptail{i}") for i in range(NL)]
        rowbuf = [SB([128, 2048], F32, f"rowbuf{i}") for i in range(2)]
        xres = [SB([128, 2048], F32, f"xres{c}") for c in range(NT)]
        xT = SB([128, NB, T], BF16, "xT")
        Y = [SB([128, NB, T], BF16, f"Y{b}") for b in range(3)]
        wslot = [SB([128, 4096], BF16, f"wslot{i}") for i in range(NSLOT)]
        wsm = SB([128, 768], BF16, "wsm")
        smalls = [SB([128, 128], F32, f"smalls{c}") for c in range(NT)]
        sexp = [SB([128, 128], F32, f"sexp{c}") for c in range(NT)]
        ss = SB([128, 8], F32, "ss")
        st6 = SB([128, 4, 6], F32, "st6")
        mv = SB([128, 2], F32, "mv")
        wgt = [SB([128, 8], F32, f"wgt{c}") for c in range(NT)]
        ARENA = 25600
        arena = P.sbuf([128, ARENA], BF16, "arena")[:]
        aoff = [0]

        pending_out = []

        def phase_begin(wait_out=False):
            if wait_out and pending_out:
                P.barrier(extra=[(sm_, sm_.count) for sm_ in pending_out])
                del pending_out[:]
            else:
                P.barrier()
            aoff[0] = 0

        def CV(shape, dt, name):
            n = 1
            for d_ in shape[1:]:
                n *= d_
            nb = n * (2 if dt == F32 else 1)
            nb = (nb + 1) // 2 * 2
            o = aoff[0]
            aoff[0] += nb
            assert aoff[0] <= ARENA, (name, aoff[0])
            ap = arena[:, o:o + nb]
            if dt == F32:
                ap = ap.bitcast(F32)
            ap = ap[:, 0:n]
            if len(shape) == 3:
                ap = ap.rearrange("p (a b) -> p a b", b=shape[2])
            return Buf(ap, name)

        mmb = [Buf(P.psum([128, 512], F32, f"mm{i}")[:], f"mm{i}", excl=True) for i in range(2)]
        smb = Buf(P.psum([128, 512], F32, "smb")[:], "smb", excl=True)
        dps = Buf(P.psum([128, 1024], F32, "dps")[:], "dps", excl=True)
        qkb = Buf(P.psum([128, 512], F32, "qkb")[:], "qkb", excl=True)
        totb = Buf(P.psum([128, 512], F32, "totb")[:], "totb", excl=True)
        trb = Buf(P.psum([128, 1024], BF16, "trb")[:], "trb", excl=True)
        mm_i = [0]

        def mm_bank():
            mm_i[0] ^= 1
            return mmb[mm_i[0]]

        ident = cst.ap[:, 0:128]
        tri_le = cst.ap[:, 128:256]
        mask_gt = cst.ap[:, 256:384]
        ones = cst.ap[:, 384:512]
        negmask = cst.ap[:, 512:640]
        icnt = cst.ap[:, 640:704]

        sp_sems = {}

        def sp_sem_for(buf):
            if id(buf) not in sp_sems:
                sp_sems[id(buf)] = P.new_sem("dma")
            return sp_sems[id(buf)]

        def sp_load(dst_buf, dst_ap, src_ap):
            P.dma("sp", dst_ap, src_ap, [dst_buf], [], sp_sem_for(dst_buf))

        wsem = [P.new_sem("wdma") for _ in range(NSLOT)]
        wsem_sw = [P.new_sem("wdma_sw") for _ in range(NSLOT)]
        wsm_sem = P.new_sem("wsm")
        w_i = [0]

        sc_bufs = {}
        wbsem = [P.new_sem("wback") for _ in range(NSLOT)]

        def load_w(kind, idx, n=4096):
            i = w_i[0] % NSLOT
            w_i[0] += 1
            b = wslot[i]
            key = (kind, idx)
            if key not in sc_bufs:
                P.dma("pool", b.ap[:, 0:n], src_d[kind][idx], [b], [], wsem_sw[i])
                sb = Buf(sc_d[kind][idx], f"sc_{kind}{idx}")
                sc_bufs[key] = sb
                P.dma("sp", sb.ap, b.ap[:, 0:n], [sb], [b], wbsem[i])
            else:
                P.dma("sp", b.ap[:, 0:n], sc_bufs[key].ap, [b], [sc_bufs[key]], wsem[i])
            return b

        def wv(b, F):
            return b.ap[:, 0:16 * F].rearrange("p (k f) -> p k f", f=F)

        sp_load(cst, cst.ap, cst_d)
        for i, l in enumerate(layers):
            sp_load(cols[i], cols[i].ap, cols_d[l])
            sp_load(srow[i], srow[i].ap, srow_d[l])
        P.dve(lambda e: e.tensor_copy(out=identb.ap, in_=ident), [identb], [cst])
        for i in range(NL):
            P.act(lambda e, i=i: e.activation(out=srow[i].ap[:, 32:64], in_=srow[i].ap[:, 32:64], func=AF.Exp), [srow[i]], [srow[i]])
            P.dve(lambda e, i=i: e.tensor_scalar(out=srow[i].ap[:, 32:64], in0=srow[i].ap[:, 32:64], scalar1=-1.0, scalar2=None, op0=ALU.mult), [srow[i]], [srow[i]])
            P.dve(lambda e, i=i: e.memset(Sst[i].ap, 0.0), [Sst[i]], [])
            for h in range(8):
                P.dve(lambda e, i=i, h=h: e.memset(Cst[i][h].ap, 0.0), [Cst[i][h]], [])
            P.dve(lambda e, i=i: e.memset(ctail[i].ap, 0.0), [ctail[i]], [])
            P.dve(lambda e, i=i: e.memset(ptail[i].ap, 0.0), [ptail[i]], [])

        def to_fm(src_buf, src_ap_fn, nblk, dst_buf, dst_ap_fn, extra_ins=()):
            j0 = 0
            while j0 < nblk:
                n = min(8, nblk - j0)
                for j in range(n):
                    P.pe(lambda e, j=j, j0=j0: e.transpose(trb.ap[:, j * 128:(j + 1) * 128], src_ap_fn(j0 + j), identb.ap),
                         [trb], [src_buf, identb], inc=(j == n - 1))
                P.act(lambda e, j0=j0, n=n: e.activation(out=dst_ap_fn(j0, n), in_=trb.ap[:, 0:n * 128].rearrange("p (a b) -> p a b", b=128), func=AF.Copy),
                      [dst_buf], [trb])
                j0 += n

        def proj_fm(wbuf, F, f0, rhs_buf, evac):
            w3 = wv(wbuf, F)
            ps = mm_bank()
            for kc in range(NB):
                P.pe(lambda e, kc=kc: e.matmul(ps.ap[:, 0:T], lhsT=w3[:, kc, f0:f0 + 128], rhs=rhs_buf.ap[:, kc, :], start=(kc == 0), stop=(kc == NB - 1)),
                     [ps], [wbuf, rhs_buf], inc=(kc == NB - 1))
            evac(ps)

        def proj_tm(wbufs, F, nf, c, lhs_buf, evac):
            ps = mm_bank()
            for wi, wbuf in enumerate(wbufs):
                w3 = wv(wbuf, F)
                for kc in range(NB):
                    P.pe(lambda e, kc=kc, w3=w3, wi=wi: e.matmul(ps.ap[:, wi * nf:(wi + 1) * nf], lhsT=lhs_buf.ap[:, kc, c * 128:(c + 1) * 128], rhs=w3[:, kc, 0:nf], start=(kc == 0), stop=(kc == NB - 1)),
                         [ps], [wbuf, lhs_buf], inc=(kc == NB - 1))
            evac(ps)

        def rstd_from(var_ap, buf, scale=1.0):
            P.act(lambda e: e.activation(out=var_ap, in_=var_ap, func=AF.Ln, bias=EPS, scale=scale), [buf], [buf])
            P.act(lambda e: e.activation(out=var_ap, in_=var_ap, func=AF.Exp, scale=-0.5), [buf], [buf])

        def run_threads(*gens):
            gens = [g_ for g_ in gens if g_ is not None]
            while gens:
                for g_ in list(gens):
                    try:
                        next(g_)
                    except StopIteration:
                        gens.remove(g_)
                    assert not P.pending["pe"]

        def layer_tile(ti, i, l, first, last):
            t0 = ti * T
            WIN = lambda key, j: ("in", l * NWBLK + BLK0[key] + j)
            cl = cols[i]
            sr = srow[i]
            P.mark(f"t{ti}l{l}:0")
            phase_begin()
            xb = CV([128, 2048], BF16, "xb")
            for c in range(NT):
                if first:
                    sp_load(xres[c], xres[c].ap, x_d[t0 + c * 128:t0 + (c + 1) * 128, :])
                P.act(lambda e, c=c: e.activation(out=xb.ap, in_=xres[c].ap, func=AF.Copy), [xb], [xres[c]])
                to_fm(xb, lambda j: xb.ap[:, j * 128:(j + 1) * 128], NB, xT,
                      lambda j0, n, c=c: xT.ap[:, j0:j0 + n, c * 128:(c + 1) * 128])

            P.dma("pool", wsm.ap, ws_d[l], [wsm], [], wsm_sem)
            wsm3 = wsm.ap.rearrange("p (k f) -> p k f", f=48)
            for c in range(NT):
                sm = smalls[c]
                for kc in range(NB):
                    P.pe(lambda e, kc=kc, c=c: e.matmul(smb.ap[:, 0:48], lhsT=xT.ap[:, kc, c * 128:(c + 1) * 128], rhs=wsm3[:, kc, :], start=(kc == 0), stop=(kc == NB - 1)),
                         [smb], [wsm, xT], inc=(kc == NB - 1))
                P.dve(lambda e, sm=sm: e.tensor_tensor(out=sm.ap[:, 0:32], in0=smb.ap[:, 0:32], in1=sr.ap[:, 0:32], op=ALU.add), [sm], [smb, sr])
                P.dve(lambda e, sm=sm: e.tensor_tensor(out=sm.ap[:, 96:104], in0=smb.ap[:, 32:40], in1=sr.ap[:, 96:104], op=ALU.add), [sm], [smb, sr])
                P.dve(lambda e, sm=sm: e.tensor_tensor(out=sm.ap[:, 104:112], in0=smb.ap[:, 40:48], in1=sr.ap[:, 104:112], op=ALU.add), [sm], [smb, sr])
                P.act(lambda e, sm=sm: e.activation(out=sm.ap[:, 0:32], in_=sm.ap[:, 0:32], func=AF.Exp), [sm], [sm])
                P.act(lambda e, sm=sm: e.activation(out=sm.ap[:, 0:32], in_=sm.ap[:, 0:32], func=AF.Ln, bias=1.0, scale=1.0), [sm], [sm])
                P.act(lambda e, sm=sm: e.activation(out=sm.ap[:, 32:64], in_=sm.ap[:, 0:32], func=AF.Ln), [sm], [sm])
                P.dve(lambda e, sm=sm: e.tensor_tensor(out=sm.ap[:, 64:96], in0=sm.ap[:, 0:32], in1=sr.ap[:, 32:64], op=ALU.mult), [sm], [sm, sr])
                P.act(lambda e, sm=sm: e.activation(out=sm.ap[:, 104:112], in_=sm.ap[:, 104:112], func=AF.Exp, scale=-1.0), [sm], [sm])
                P.act(lambda e, sm=sm: e.activation(out=sm.ap[:, 104:112], in_=sm.ap[:, 104:112], func=AF.Ln, bias=1.0, scale=1.0), [sm], [sm])
                P.dve(lambda e, sm=sm: e.tensor_scalar(out=sm.ap[:, 104:112], in0=sm.ap[:, 104:112], scalar1=-1.0, scalar2=None, op0=ALU.mult), [sm], [sm])
                a_ap = sm.ap[:, 64:96]
                P.pe(lambda e, a_ap=a_ap: e.matmul(smb.ap[:, 0:32], lhsT=tri_le, rhs=a_ap, start=True, stop=True), [smb], [cst, sm], inc=False)
                P.pe(lambda e, a_ap=a_ap: e.matmul(smb.ap[:, 32:64], lhsT=mask_gt, rhs=a_ap, start=True, stop=True), [smb], [cst, sm], inc=False)
                P.pe(lambda e, a_ap=a_ap: e.matmul(smb.ap[:, 64:96], lhsT=ones, rhs=a_ap, start=True, stop=True), [smb], [cst, sm], inc=True)
                P.act(lambda e, c=c: e.activation(out=sexp[c].ap[:, 0:96], in_=smb.ap[:, 0:96], func=AF.Exp), [sexp[c]], [smb])
                P.dve(lambda e, c=c, sm=sm: e.tensor_tensor(out=sexp[c].ap[:, 96:128], in0=sexp[c].ap[:, 32:64], in1=sm.ap[:, 0:32], op=ALU.mult), [sexp[c]], [sexp[c], sm])

            P.mark(f"t{ti}l{l}:A")
            phase_begin(wait_out=True)
            pA = CV([128, 16 + T], F32, "pA")
            pB = CV([128, 16 + T], F32, "pB")
            pC = CV([128, 16 + T], F32, "pC")
            p16 = CV([128, 16], F32, "p16")
            pooled = CV([128, 4, T], BF16, "pooled")
            spz = CV([128, 4, T], BF16, "spz")
            for g in range(4):
                nsteps = g + 1
                wwin = 2 ** (g + 1)
                for jb in range(2):
                    wu = load_w(*WIN("pu", g * 2 + jb))
                    wz = load_w(*WIN("pz", g * 2 + jb))
                    for m in range(2):
                        q = jb * 2 + m
                        blk = g * 4 + q

                        def evac_pu(ps, q=q, blk=blk):
                            if ti == 0:
                                P.dve(lambda e: e.memset(pA.ap[:, 0:16], 0.0), [pA], [])
                            else:
                                P.dve(lambda e: e.tensor_copy(out=pA.ap[:, 0:16], in_=ptail[i].ap[:, blk, :]), [pA], [ptail[i]])
                            P.act(lambda e: e.activation(out=pA.ap[:, 16:16 + T], in_=ps.ap[:, 0:T], func=AF.Copy), [pA], [ps])
                            P.dve(lambda e: e.tensor_copy(out=ptail[i].ap[:, blk, :], in_=pA.ap[:, T:T + 16]), [ptail[i]], [pA])
                            cur = pA
                            lo = 0
                            for k in range(nsteps):
                                stp = 2 ** k
                                lo += stp
                                dst = pB if k % 2 == 0 else pC
                                P.dve(lambda e, cur=cur, dst=dst, lo=lo, stp=stp: e.tensor_tensor(out=dst.ap[:, lo:16 + T], in0=cur.ap[:, lo:16 + T], in1=cur.ap[:, lo - stp:16 + T - stp], op=ALU.add),
                                      [dst], [cur])
                                cur = dst
                            P.dve(lambda e, cur=cur: e.scalar_tensor_tensor(out=pooled.ap[:, q, :], in0=cur.ap[:, 16:16 + T], scalar=1.0 / wwin, in1=pA.ap[:, 16:16 + T], op0=ALU.mult, op1=ALU.subtract),
                                  [pooled], [cur, pA])
                            if ti == 0:
                                P.dve(lambda e, cur=cur: e.tensor_tensor(out=p16.ap, in0=cur.ap[:, 16:32], in1=icnt[:, g * 16:(g + 1) * 16], op=ALU.mult), [p16], [cur, cst])
                                P.dve(lambda e: e.tensor_tensor(out=pooled.ap[:, q, 0:16], in0=p16.ap, in1=pA.ap[:, 16:32], op=ALU.subtract), [pooled], [p16, pA])

                        proj_fm(wu, 256, m * 128, xT, evac_pu)
                        proj_fm(wz, 256, m * 128, xT,
                                lambda ps, q=q: P.act(lambda e: e.activation(out=spz.ap[:, q, :], in_=ps.ap[:, 0:T], func=AF.Silu), [spz], [ps]))
                wp = load_w("p", l * 4 + g, 2048)
                wp3 = wp.ap[:, 0:2048].rearrange("p (k f) -> p k f", f=512)
                for m in range(4):
                    ps = mm_bank()
                    for kc in range(4):
                        P.pe(lambda e, kc=kc, m=m, ps=ps: e.matmul(ps.ap[:, 0:T], lhsT=wp3[:, kc, m * 128:(m + 1) * 128], rhs=pooled.ap[:, kc, :], start=(kc == 0), stop=(kc == 3)),
                             [ps], [wp, pooled], inc=(kc == 3))
                    P.dve(lambda e, m=m, ps=ps: e.scalar_tensor_tensor(out=Y[0].ap[:, g * 4 + m, :], in0=ps.ap[:, 0:T], scalar=cl.ap[:, 48 + g * 4 + m:48 + g * 4 + m + 1], in1=spz.ap[:, m, :], op0=ALU.mult, op1=ALU.mult),
                          [Y[0]], [ps, cl, spz])

            P.mark(f"t{ti}l{l}:B")
            phase_begin()
            craw2 = [CV([128, 4 + T], F32, f"craw{k_}") for k_ in range(2)]
            cacc2 = [CV([128, T], F32, f"cacc{k_}") for k_ in range(2)]
            cv_i = [0]
            cv_pend = []

            def conv_flush():
                while cv_pend:
                    cb_, oap_, ob_ = cv_pend.pop(0)
                    P.act(lambda e: e.activation(out=oap_, in_=cb_.ap, func=AF.Silu), [ob_], [cb_])
            xfm = CV([128, 4, T], BF16, "xfm")
            BT = [CV([128, T], BF16, f"BT{k_}") for k_ in range(4)]
            CT = [CV([128, T], BF16, f"CT{k_}") for k_ in range(4)]
            Xtok2 = [CV([128, NT, 512], BF16, f"Xtok{k_}") for k_ in range(2)]
            Btok2 = [CV([128, NT, 128], BF16, f"Btok{k_}") for k_ in range(2)]
            szg2 = [CV([128, NT, 512], F32, f"szg{k_}") for k_ in range(2)]
            LT = CV([128, 8, 128], F32, "LT")
            ADDt = CV([128, 8, 128], F32, "ADDt")
            MT4 = [CV([128, 8, 128], BF16, f"MT{k_}") for k_ in range(2 * NT)]
            Sbf = CV([128, 512], BF16, "Sbf")
            y1 = CV([128, 512], F32, "y1")
            y2 = CV([128, 512], F32, "y2")
            y3 = CV([128, 512], F32, "y3")
            ytok = CV([128, 512], BF16, "ytok")
            Xd = CV([128, 512], BF16, "Xd")
            sp_load(rowbuf[0], rowbuf[0].ap, rows_d[l * 4 + 0])

            def conv_block(ps, cb, out_ap, out_buf):
                cv_i[0] ^= 1
                craw, cacc = craw2[cv_i[0]], cacc2[cv_i[0]]
                P.dve(lambda e: e.tensor_copy(out=craw.ap[:, 0:3], in_=ctail[i].ap[:, cb, :]), [craw], [ctail[i]])
                P.act(lambda e: e.activation(out=craw.ap[:, 3:3 + T], in_=ps.ap[:, 0:T], func=AF.Copy), [craw], [ps])
                P.dve(lambda e: e.tensor_copy(out=ctail[i].ap[:, cb, :], in_=craw.ap[:, T:T + 3]), [ctail[i]], [craw])
                wcol = lambda j: cl.ap[:, 64 + j * 24 + cb:64 + j * 24 + cb + 1]
                bcol = cl.ap[:, 160 + cb:160 + cb + 1]
                P.dve(lambda e: e.tensor_scalar(out=cacc.ap, in0=craw.ap[:, 3:3 + T], scalar1=wcol(3), scalar2=bcol, op0=ALU.mult, op1=ALU.add), [cacc], [craw, cl])
                for j in range(3):
                    P.dve(lambda e, j=j: e.scalar_tensor_tensor(out=cacc.ap, in0=craw.ap[:, j:j + T], scalar=wcol(j), in1=cacc.ap, op0=ALU.mult, op1=ALU.add), [cacc], [craw, cl, cacc])
                conv_flush()
                cv_pend.append((cacc, out_ap, out_buf))

            v3 = lambda ap: ap.rearrange("p (a b) -> p a b", b=64)

            def gen_proj_BC(g):
                if g < 4 and g % 2 == 0:
                    wB = load_w(*WIN("sB", g // 2))
                    wC = load_w(*WIN("sC", g // 2))
                    for m in range(2):
                        proj_fm(wB, 256, m * 128, xT, lambda ps, m=m: conv_block(ps, 16 + g + m, BT[g + m].ap, BT[g + m]))
                        yield
                        proj_fm(wC, 256, m * 128, xT, lambda ps, m=m: conv_block(ps, 20 + g + m, CT[g + m].ap, CT[g + m]))
                        if m == 1:
                            conv_flush()
                        yield

            def gen_proj_X(g):
                pb = g % 2
                Xtok, Btok, szg = Xtok2[pb], Btok2[pb], szg2[pb]
                for jb in range(2):
                    wx = load_w(*WIN("sx", g * 2 + jb))
                    for m in range(2):
                        q = jb * 2 + m
                        proj_fm(wx, 256, m * 128, xT, lambda ps, q=q: conv_block(ps, g * 4 + q, xfm.ap[:, q, :], xfm))
                        if q == 3:
                            conv_flush()
                        yield
                for c in range(NT):
                    to_fm(xfm, lambda j, c=c: xfm.ap[:, j, c * 128:(c + 1) * 128], 4, Xtok,
                          lambda j0, n, c=c: Xtok.ap[:, c, :].rearrange("p (a b) -> p a b", b=128))
                    to_fm(BT[g], lambda j, c=c: BT[g].ap[:, c * 128:(c + 1) * 128], 1, Btok,
                          lambda j0, n, c=c: Btok.ap[:, c, :].rearrange("p (a b) -> p a b", b=128))
                    yield
                wz0 = load_w(*WIN("sz", g * 2))
                wz1 = load_w(*WIN("sz", g * 2 + 1))
                for c in range(NT):
                    proj_tm([wz0, wz1], 256, 256, c, xT,
                            lambda ps, c=c: P.act(lambda e: e.activation(out=szg.ap[:, c, :], in_=ps.ap[:, 0:512], func=AF.Silu), [szg], [ps]))
                    yield

            def gen_front_B(g):
                BTg, CTg = BT[g], CT[g]
                for c in range(NT):
                    MT = MT4[(g % 2) * NT + c]
                    sm = smalls[c]
                    a_g = sm.ap[:, 64 + g * 8:64 + (g + 1) * 8]
                    lndt_g = sm.ap[:, 32 + g * 8:32 + (g + 1) * 8]
                    cs = slice(c * 128, (c + 1) * 128)
                    P.dve(lambda e: e.tensor_tensor(out=LT.ap, in0=bc(tri_le, 1, [128, 8, 128]), in1=bc(a_g, 2, [128, 8, 128]), op=ALU.mult), [LT], [cst, sm])
                    P.dve(lambda e: e.tensor_tensor(out=ADDt.ap, in0=bc(negmask, 1, [128, 8, 128]), in1=bc(lndt_g, 2, [128, 8, 128]), op=ALU.add), [ADDt], [cst, sm])
                    LT2 = LT.ap.rearrange("p a b -> p (a b)")
                    AD2 = ADDt.ap.rearrange("p a b -> p (a b)")
                    for hh in range(2):
                        P.pe(lambda e: e.matmul(dps.ap[:, hh * 512:(hh + 1) * 512], lhsT=mask_gt, rhs=LT2[:, hh * 512:(hh + 1) * 512], start=True, stop=False), [dps], [cst, LT], inc=False)
                        P.pe(lambda e: e.matmul(dps.ap[:, hh * 512:(hh + 1) * 512], lhsT=ident, rhs=AD2[:, hh * 512:(hh + 1) * 512], start=False, stop=True), [dps], [cst, ADDt], inc=(hh == 1))
                    P.pe(lambda e: e.matmul(qkb.ap[:, 0:128], lhsT=BTg.ap[:, cs], rhs=CTg.ap[:, cs], start=True, stop=True), [qkb], [BTg, CTg])
                    yield
                    P.act(lambda e: e.activation(out=LT2, in_=dps.ap, func=AF.Exp), [LT], [dps])
                    P.dve(lambda e: e.tensor_tensor(out=MT.ap, in0=LT.ap, in1=bc(qkb.ap[:, 0:128], 1, [128, 8, 128]), op=ALU.mult), [MT], [LT, qkb])
                    yield

            def gen_back_B(g):
                pb = g % 2
                Xtok, Btok, szg = Xtok2[pb], Btok2[pb], szg2[pb]
                CTg = CT[g]
                Sg = Sst[i].ap[:, g * 512:(g + 1) * 512]
                for c in range(NT):
                    MT = MT4[(g % 2) * NT + c]
                    se = sexp[c]
                    eA_g = se.ap[:, g * 8:(g + 1) * 8]
                    cd_g = se.ap[:, 64 + g * 8:64 + (g + 1) * 8]
                    dd_g = se.ap[:, 96 + g * 8:96 + (g + 1) * 8]
                    cs = slice(c * 128, (c + 1) * 128)
                    P.dve(lambda e: e.tensor_copy(out=Sbf.ap, in_=Sg), [Sbf], [Sst[i]])
                    P.pe(lambda e: e.matmul(smb.ap[:, 0:512], lhsT=CTg.ap[:, cs], rhs=Sbf.ap, start=True, stop=True), [smb], [CTg, Sbf])
                    for h in range(8):
                        P.pe(lambda e: e.matmul(totb.ap[:, h * 64:(h + 1) * 64], lhsT=MT.ap[:, h, :], rhs=Xtok.ap[:, c, h * 64:(h + 1) * 64], start=True, stop=True), [totb], [MT, Xtok], inc=(h == 7))
                    P.dve(lambda e: e.tensor_tensor(out=v3(Xd.ap), in0=v3(Xtok.ap[:, c, :]), in1=bc(dd_g, 2, [128, 8, 64]), op=ALU.mult), [Xd], [Xtok, se])
                    yield
                    P.dve(lambda e: e.tensor_tensor(out=v3(y1.ap), in0=v3(smb.ap[:, 0:512]), in1=bc(eA_g, 2, [128, 8, 64]), op=ALU.mult), [y1], [smb, se])
                    P.pe(lambda e: e.matmul(smb.ap[:, 0:512], lhsT=Btok.ap[:, c, :], rhs=Xd.ap, start=True, stop=True), [smb], [Btok, Xd])
                    P.dve(lambda e: e.tensor_tensor(out=y2.ap, in0=totb.ap[:, 0:512], in1=y1.ap, op=ALU.add), [y2], [totb, y1])
                    dsk = sr.ap[:, 64 + g * 8:64 + (g + 1) * 8]
                    P.dve(lambda e: e.tensor_tensor(out=v3(y1.ap), in0=v3(Xtok.ap[:, c, :]), in1=bc(dsk, 2, [128, 8, 64]), op=ALU.mult), [y1], [Xtok, sr])
                    P.dve(lambda e: e.tensor_tensor(out=y2.ap, in0=y2.ap, in1=y1.ap, op=ALU.add), [y2], [y2, y1])
                    P.dve(lambda e: e.tensor_tensor(out=y2.ap, in0=y2.ap, in1=szg.ap[:, c, :], op=ALU.mult), [y2], [y2, szg])
                    P.act(lambda e: e.activation(out=y1.ap, in_=y2.ap, func=AF.Square, accum_out=ss.ap[:, 0:1]), [y1, ss], [y2])
                    P.dve(lambda e: e.tensor_tensor(out=v3(y3.ap), in0=v3(Sg), in1=bc(cd_g, 2, [128, 8, 64]), op=ALU.mult), [y3], [Sst[i], se])
                    P.dve(lambda e: e.tensor_tensor(out=Sg, in0=smb.ap[:, 0:512], in1=y3.ap, op=ALU.add), [Sst[i]], [smb, y3])
                    yield
                    rstd_from(ss.ap[:, 0:1], ss, scale=1.0 / 512)
                    P.dve(lambda e: e.scalar_tensor_tensor(out=ytok.ap, in0=y2.ap, scalar=ss.ap[:, 0:1], in1=rowbuf[0].ap[:, g * 512:(g + 1) * 512], op0=ALU.mult, op1=ALU.mult), [ytok], [y2, ss, rowbuf[0]])
                    to_fm(ytok, lambda j: ytok.ap[:, j * 128:(j + 1) * 128], 4, Y[1],
                          lambda j0, n: Y[1].ap[:, g * 4:g * 4 + 4, c * 128:(c + 1) * 128])
                    yield

            def g_seq(*gens):
                for g_ in gens:
                    if g_ is not None:
                        yield from g_

            def g_par(*gens):
                gens = [g_ for g_ in gens if g_ is not None]
                while gens:
                    for g_ in list(gens):
                        try:
                            next(g_)
                            yield
                        except StopIteration:
                            gens.remove(g_)

            run_threads(gen_proj_BC(0))
            run_threads(gen_front_B(0), gen_proj_X(0))
            for g in range(4):
                if g < 3:
                    run_threads(gen_back_B(g), g_seq(gen_proj_BC(g + 1), g_par(gen_front_B(g + 1), gen_proj_X(g + 1))))
                else:
                    run_threads(gen_back_B(g))

            P.mark(f"t{ti}l{l}:C")
            phase_begin()
            EB = [CV([128, 8, 128], F32, f"EB{c}") for c in range(NT)]
            EE = [CV([128, 8, 128], F32, f"EE{c}") for c in range(NT)]
            PT = CV([128, 128], BF16, "PT")
            qp = CV([128, 2, 128], BF16, "qp")
            Cbf = CV([128, 2, 258], BF16, "Cbf")
            hc = CV([128, 256], F32, "hc")
            hy = CV([128, 256], F32, "hy")
            hyb = CV([128, 256], BF16, "hyb")

            def cset(k_):
                return dict(qT=CV([128, 2, T], BF16, f"qT{k_}"), kT=CV([128, 2, T], BF16, f"kT{k_}"),
                            vaug=CV([128, NT, 258], BF16, f"vaug{k_}"), sigo=CV([128, NT, 256], F32, f"sigo{k_}"),
                            silz=CV([128, NT, 256], F32, f"silz{k_}"), kw=CV([128, NT, 256], BF16, f"kw{k_}"))

            cs0 = cset(0)
            ov = aoff[0]
            LT = CV([128, 8, 128], F32, "LT")
            ADDt = CV([128, 8, 128], F32, "ADDt")
            sp_load(rowbuf[1], rowbuf[1].ap, rows_d[l * 4 + 1])
            for c in range(NT):
                sm = smalls[c]
                ig = sm.ap[:, 96:104]
                lf = sm.ap[:, 104:112]
                P.pe(lambda e: e.matmul(smb.ap[:, 0:8], lhsT=mask_gt, rhs=lf, start=True, stop=False), [smb], [cst, sm], inc=False)
                P.pe(lambda e: e.matmul(smb.ap[:, 0:8], lhsT=ident, rhs=ig, start=False, stop=True), [smb], [cst, sm])
                P.act(lambda e: e.activation(out=wgt[c].ap, in_=smb.ap[:, 0:8], func=AF.Exp), [wgt[c]], [smb])
                P.dve(lambda e: e.tensor_tensor(out=LT.ap, in0=bc(tri_le, 1, [128, 8, 128]), in1=bc(lf, 2, [128, 8, 128]), op=ALU.mult), [LT], [cst, sm])
                P.dve(lambda e: e.tensor_tensor(out=ADDt.ap, in0=bc(negmask, 1, [128, 8, 128]), in1=bc(ig, 2, [128, 8, 128]), op=ALU.add), [ADDt], [cst, sm])
                LT2 = LT.ap.rearrange("p a b -> p (a b)")
                AD2 = ADDt.ap.rearrange("p a b -> p (a b)")
                for hh in range(2):
                    P.pe(lambda e: e.matmul(dps.ap[:, hh * 512:(hh + 1) * 512], lhsT=ones, rhs=LT2[:, hh * 512:(hh + 1) * 512], start=True, stop=True), [dps], [cst, LT], inc=(hh == 1))
                P.act(lambda e: e.activation(out=EB[c].ap.rearrange("p a b -> p (a b)"), in_=dps.ap, func=AF.Exp), [EB[c]], [dps])
                for hh in range(2):
                    P.pe(lambda e: e.matmul(dps.ap[:, hh * 512:(hh + 1) * 512], lhsT=mask_gt, rhs=LT2[:, hh * 512:(hh + 1) * 512], start=True, stop=False), [dps], [cst, LT], inc=False)
                    P.pe(lambda e: e.matmul(dps.ap[:, hh * 512:(hh + 1) * 512], lhsT=ident, rhs=AD2[:, hh * 512:(hh + 1) * 512], start=False, stop=True), [dps], [cst, ADDt], inc=(hh == 1))
                P.act(lambda e: e.activation(out=EE[c].ap.rearrange("p a b -> p (a b)"), in_=dps.ap, func=AF.Exp), [EE[c]], [dps])
            P.barrier()
            aoff[0] = ov
            cs1 = cset(1)
            csets = [cs0, cs1]
            for k_ in range(2):
                P.dve(lambda e: e.memset(csets[k_]["vaug"].ap[:, :, 256:257], 1.0), [csets[k_]["vaug"]], [])

            def gen_proj_C(h):
                S_ = csets[h % 2]
                qT, kT, vaug, sigo, silz, kw = S_["qT"], S_["kT"], S_["vaug"], S_["sigo"], S_["silz"], S_["kw"]
                wq = load_w(*WIN("mq", h))
                wk = load_w(*WIN("mk", h))
                for m in range(2):
                    proj_fm(wq, 256, m * 128, xT, lambda ps, m=m: P.act(lambda e: e.activation(out=qT.ap[:, m, :], in_=ps.ap[:, 0:T], func=AF.Copy), [qT], [ps]))
                    yield
                    proj_fm(wk, 256, m * 128, xT, lambda ps, m=m: P.act(lambda e: e.activation(out=kT.ap[:, m, :], in_=ps.ap[:, 0:T], func=AF.Copy, scale=0.0625), [kT], [ps]))
                    yield
                wvv = load_w(*WIN("mv", h))
                wo_ = load_w(*WIN("mo", h))
                for c in range(NT):
                    proj_tm([wvv], 256, 256, c, xT, lambda ps, c=c: P.act(lambda e: e.activation(out=vaug.ap[:, c, 0:256], in_=ps.ap[:, 0:256], func=AF.Copy), [vaug], [ps]))
                    yield
                    proj_tm([wo_], 256, 256, c, xT, lambda ps, c=c: P.act(lambda e: e.activation(out=sigo.ap[:, c, :], in_=ps.ap[:, 0:256], func=AF.Sigmoid), [sigo], [ps]))
                    yield
                wzz = load_w(*WIN("mz", h))
                for c in range(NT):
                    proj_tm([wzz], 256, 256, c, xT, lambda ps, c=c: P.act(lambda e: e.activation(out=silz.ap[:, c, :], in_=ps.ap[:, 0:256], func=AF.Silu), [silz], [ps]))
                    P.dve(lambda e: e.tensor_tensor(out=silz.ap[:, c, :], in0=silz.ap[:, c, :], in1=rowbuf[1].ap[:, h * 256:(h + 1) * 256], op=ALU.mult), [silz], [silz, rowbuf[1]])
                    yield
                for c in range(NT):
                    cs = slice(c * 128, (c + 1) * 128)
                    for m in range(2):
                        P.pe(lambda e: e.transpose(trb.ap[:, m * 128:(m + 1) * 128], kT.ap[:, m, cs], identb.ap), [trb], [kT, identb], inc=(m == 1))
                    P.dve(lambda e: e.tensor_scalar(out=kw.ap[:, c, :], in0=trb.ap[:, 0:256], scalar1=wgt[c].ap[:, h:h + 1], scalar2=None, op0=ALU.mult), [kw], [trb, wgt[c]])
                    yield

            def gen_core_C(h):
                S_ = csets[h % 2]
                qT, kT, vaug, sigo, silz, kw = S_["qT"], S_["kT"], S_["vaug"], S_["sigo"], S_["silz"], S_["kw"]
                for c in range(NT):
                    cs = slice(c * 128, (c + 1) * 128)
                    for m in range(2):
                        P.pe(lambda e: e.matmul(qkb.ap[:, 0:128], lhsT=kT.ap[:, m, cs], rhs=qT.ap[:, m, cs], start=(m == 0), stop=(m == 1)), [qkb], [kT, qT], inc=(m == 1))
                    P.dve(lambda e: e.tensor_tensor(out=qp.ap, in0=qT.ap[:, :, cs], in1=bc(EB[c].ap[:, h, :], 1, [128, 2, 128]), op=ALU.mult), [qp], [qT, EB[c]])
                    P.dve(lambda e: e.tensor_copy(out=Cbf.ap[:, :, 0:257], in_=Cst[i][h].ap), [Cbf], [Cst[i][h]])
                    yield
                    P.dve(lambda e: e.tensor_tensor(out=PT.ap, in0=qkb.ap[:, 0:128], in1=EE[c].ap[:, h, :], op=ALU.mult), [PT], [qkb, EE[c]])
                    P.pe(lambda e: e.matmul(totb.ap[:, 0:257], lhsT=PT.ap, rhs=vaug.ap[:, c, 0:257], start=True, stop=False), [totb], [PT, vaug], inc=False)
                    for m in range(2):
                        P.pe(lambda e: e.matmul(totb.ap[:, 0:257], lhsT=qp.ap[:, m, :], rhs=Cbf.ap[:, m, 0:257], start=False, stop=(m == 1)), [totb], [qp, Cbf], inc=(m == 1))
                    for m in range(2):
                        P.pe(lambda e: e.matmul(dps.ap[:, m * 512:m * 512 + 257], lhsT=kw.ap[:, c, m * 128:(m + 1) * 128], rhs=vaug.ap[:, c, 0:257], start=True, stop=True), [dps], [kw, vaug], inc=(m == 1))
                    yield
                    P.dve(lambda e: e.tensor_tensor(out=hc.ap, in0=totb.ap[:, 0:256], in1=sigo.ap[:, c, :], op=ALU.mult), [hc], [totb, sigo])
                    P.act(lambda e: e.activation(out=ss.ap[:, 1:2], in_=totb.ap[:, 256:257], func=AF.Square), [ss], [totb])
                    P.dve(lambda e: e.bn_stats(out=st6.ap[:, 0, :], in_=hc.ap), [st6], [hc])
                    P.dve(lambda e: e.bn_aggr(out=mv.ap, in_=st6.ap[:, 0, :]), [mv], [st6])
                    P.dve(lambda e: e.tensor_scalar(out=ss.ap[:, 1:2], in0=ss.ap[:, 1:2], scalar1=1.0, scalar2=EPS, op0=ALU.max, op1=ALU.mult), [ss], [ss])
                    P.act(lambda e: e.activation(out=mv.ap[:, 1:2], in_=mv.ap[:, 1:2], func=AF.Ln, bias=ss.ap[:, 1:2], scale=1.0), [mv], [mv, ss])
                    for m in range(2):
                        P.dve(lambda e: e.scalar_tensor_tensor(out=Cst[i][h].ap[:, m, :], in0=Cst[i][h].ap[:, m, :], scalar=EB[c].ap[:, h, 127:128], in1=dps.ap[:, m * 512:m * 512 + 257], op0=ALU.mult, op1=ALU.add),
                              [Cst[i][h]], [Cst[i][h], EB[c], dps])
                    P.act(lambda e: e.activation(out=mv.ap[:, 1:2], in_=mv.ap[:, 1:2], func=AF.Exp, scale=-0.5), [mv], [mv])
                    P.dve(lambda e: e.tensor_scalar(out=hy.ap, in0=hc.ap, scalar1=mv.ap[:, 0:1], scalar2=mv.ap[:, 1:2], op0=ALU.subtract, op1=ALU.mult), [hy], [hc, mv])
                    P.dve(lambda e: e.tensor_tensor(out=hyb.ap, in0=hy.ap, in1=silz.ap[:, c, :], op=ALU.mult), [hyb], [hy, silz])
                    to_fm(hyb, lambda j: hyb.ap[:, j * 128:(j + 1) * 128], 2, Y[2],
                          lambda j0, n: Y[2].ap[:, h * 2:h * 2 + 2, c * 128:(c + 1) * 128])
                    yield

            run_threads(gen_proj_C(0))
            for h in range(8):
                run_threads(gen_core_C(h), gen_proj_C(h + 1) if h < 7 else None)

            P.mark(f"t{ti}l{l}:D")
            phase_begin()
            _rsv = CV([128, 4096], BF16, "rsv")
            ostage = [CV([128, 2048], F32, f"ostage{c}") for c in range(NT)] if last else None
            mergedT = CV([128, NB, T], BF16, "mergedT")
            gsb = CV([128, T], F32, "gsb")
            gtmp = CV([128, T], F32, "gtmp")
            macc = [CV([128, T], F32, f"macc{m}") for m in range(2)]
            sp_load(rowbuf[0], rowbuf[0].ap, rows_d[l * 4 + 2])
            sp_load(rowbuf[1], rowbuf[1].ap, rows_d[l * 4 + 3])
            for j8 in range(8):
                for b in range(3):
                    wg = load_w(*WIN("g", b * 8 + j8))
                    wbb = load_w("b", (l * 3 + b) * 8 + j8)
                    for m in range(2):
                        j = j8 * 2 + m
                        proj_fm(wg, 256, m * 128, xT,
                                lambda ps, j=j, b=b: P.act(lambda e: e.activation(out=gsb.ap, in_=ps.ap[:, 0:T], func=AF.Sigmoid, bias=cl.ap[:, b * 16 + j:b * 16 + j + 1], scale=1.0), [gsb], [ps, cl]))

                        def evac_p(ps, j=j, b=b, m=m):
                            if b == 0:
                                P.dve(lambda e: e.tensor_tensor(out=macc[m].ap, in0=ps.ap[:, 0:T], in1=gsb.ap, op=ALU.mult), [macc[m]], [ps, gsb])
                            else:
                                P.dve(lambda e: e.tensor_tensor(out=gtmp.ap, in0=ps.ap[:, 0:T], in1=gsb.ap, op=ALU.mult), [gtmp], [ps, gsb])
                                if b == 1:
                                    P.dve(lambda e: e.tensor_tensor(out=macc[m].ap, in0=macc[m].ap, in1=gtmp.ap, op=ALU.add), [macc[m]], [macc[m], gtmp])
                                else:
                                    P.dve(lambda e: e.tensor_tensor(out=mergedT.ap[:, j, :], in0=macc[m].ap, in1=gtmp.ap, op=ALU.add), [mergedT], [macc[m], gtmp])

                        proj_fm(wbb, 256, m * 128, Y[b], evac_p)
            for n8 in range(8):
                wo = load_w("o", l * 8 + n8)
                for c in range(NT):
                    proj_tm([wo], 256, 256, c, mergedT,
                            lambda ps, c=c, n8=n8: P.dve(lambda e: e.scalar_tensor_tensor(out=xres[c].ap[:, n8 * 256:(n8 + 1) * 256], in0=xres[c].ap[:, n8 * 256:(n8 + 1) * 256], scalar=ALPHA, in1=ps.ap[:, 0:256], op0=ALU.mult, op1=ALU.add),
                                                         [xres[c]], [xres[c], ps]))
            for c in range(NT):
                xr = xres[c]
                for s4 in range(4):
                    P.dve(lambda e, s4=s4, xr=xr: e.bn_stats(out=st6.ap[:, s4, :], in_=xr.ap[:, s4 * 512:(s4 + 1) * 512]), [st6], [xr])
                P.dve(lambda e: e.bn_aggr(out=mv.ap, in_=st6.ap.rearrange("p a b -> p (a b)")), [mv], [st6])
                rstd_from(mv.ap[:, 1:2], mv)
                P.dve(lambda e, xr=xr: e.tensor_scalar(out=xr.ap, in0=xr.ap, scalar1=mv.ap[:, 0:1], scalar2=mv.ap[:, 1:2], op0=ALU.subtract, op1=ALU.mult), [xr], [xr, mv])
                P.dve(lambda e, xr=xr: e.tensor_tensor(out=xr.ap, in0=xr.ap, in1=rowbuf[0].ap, op=ALU.mult), [xr], [xr, rowbuf[0]])
                if last:
                    og = ostage[c]
                    P.dve(lambda e, xr=xr: e.tensor_tensor(out=og.ap, in0=xr.ap, in1=rowbuf[1].ap, op=ALU.add), [og], [xr, rowbuf[1]])
                    ob = Buf(out_d[t0 + c * 128:t0 + (c + 1) * 128, :])
                    P.dma("sp", ob.ap, og.ap, [ob], [og], st_sems[c])
                    out_bufs.append(ob)
                    if st_sems[c] not in pending_out:
                        pending_out.append(st_sems[c])
                else:
                    P.dve(lambda e, xr=xr: e.tensor_tensor(out=xr.ap, in0=xr.ap, in1=rowbuf[1].ap, op=ALU.add), [xr], [xr, rowbuf[1]])

        out_bufs = []
        st_sems = [P.new_sem("store") for _ in range(NT)]
        for ti in range(NTILES):
            for i, l in enumerate(layers):
                layer_tile(ti, i, l, first=(i == 0), last=(i == NL - 1))
        P.mark("end")
        P.wait_for("sp", out_bufs)
        P.finish()
    nc._marks = P.marks
    nc._tags = P.tags
    nc._names = P.names
    return nc


def _blk(w, c0):
    return np.ascontiguousarray(w[:, c0:c0 + 256].reshape(16, 128, 256).transpose(1, 0, 2)).reshape(128, 4096)


def prep_weights(inp):
    L = DEPTH
    w_in = inp["w_in"]
    w_in_r = np.empty((L * NWBLK, 128, 4096), np.float32)
    ws_r = np.empty((L, 128, 768), np.float32)
    wb_r = np.empty((L * 24, 128, 4096), np.float32)
    wo_r = np.empty((L * 8, 128, 4096), np.float32)
    wp_r = np.empty((L * 4, 128, 2048), np.float32)
    cols_l = np.empty((L, 128, 184), np.float32)
    srow_l = np.empty((L, 128, 112), np.float32)
    rows_l = np.empty((L * 4, 128, 2048), np.float32)
    small_cols = np.concatenate([np.arange(9216, 9248), np.arange(19488, 19504)])
    for l in range(L):
        for key in SEC:
            for j in range(SECN[key]):
                w_in_r[l * NWBLK + BLK0[key] + j] = _blk(w_in[l], SEC[key] + j * 256)
        ws = w_in[l][:, small_cols]
        ws_r[l] = ws.reshape(16, 128, 48).transpose(1, 0, 2).reshape(128, 768)
        for b in range(3):
            for j8 in range(8):
                wb_r[(l * 3 + b) * 8 + j8] = _blk(inp["w_branch"][l, b], j8 * 256)
        for n8 in range(8):
            wo_r[l * 8 + n8] = _blk(inp["w_out"][l], n8 * 256)
        for g in range(4):
            wp_r[l * 4 + g] = inp["w_pool"][l, g].reshape(4, 128, 512).transpose(1, 0, 2).reshape(128, 2048)
        cols_l[l, :, 0:48] = inp["b_gate"][l].reshape(3, 16, 128).transpose(2, 0, 1).reshape(128, 48)
        cols_l[l, :, 48:64] = inp["pool_scale"][l].reshape(16, 128).T
        cols_l[l, :, 64:160] = inp["conv_w"][l].reshape(4, 24, 128).transpose(2, 0, 1).reshape(128, 96)
        cols_l[l, :, 160:184] = inp["conv_b"][l].reshape(24, 128).T
        srow = np.concatenate([inp["dt_bias"][l], inp["a_log"][l], inp["d_skip"][l], inp["i_bias"][l], inp["f_bias"][l]])
        srow_l[l] = np.broadcast_to(srow[None, :], (128, 112))
        for k, name in enumerate(["ssm_norm_w", "mlstm_norm_w", "ln_g", "ln_b"]):
            rows_l[l * 4 + k] = np.broadcast_to(inp[name][l][None, :], (128, 2048))
    r = np.arange(128)
    consts = np.zeros((128, 704), np.float32)
    consts[:, 0:128] = np.eye(128)
    consts[:, 128:256] = (r[:, None] <= r[None, :])
    consts[:, 256:384] = (r[:, None] > r[None, :])
    consts[:, 384:512] = 1.0
    consts[:, 512:640] = np.where(r[None, :] < r[:, None], -30000.0, 0.0)
    for g in range(4):
        w = 2 ** (g + 1)
        consts[:, 640 + g * 16:640 + (g + 1) * 16] = 1.0 / np.minimum(np.arange(16) + 1, w)[None, :]
    return {"w_in_r": w_in_r, "ws_r": ws_r, "wb_r": wb_r, "wo_r": wo_r, "wp_r": wp_r, "cols_l": cols_l,
            "srow_l": srow_l, "rows_l": rows_l, "consts": consts}


TILE_T = 256


def kernel(**inputs):
    inp = {k: np.asarray(v, dtype=np.float32) for k, v in inputs.items()}
    x = inp["x"]
    shared = prep_weights(inp)
    nc = build_nc(list(range(DEPTH)), TILE_T)
    in_maps = []
    for b in range(8):
        m = dict(shared)
        m["x"] = np.ascontiguousarray(x[b])
        in_maps.append(m)
    res = run_bass_kernel_spmd(nc, in_maps, core_ids=list(range(8)))
    return np.stack([np.asarray(r["out"], dtype=np.float32) for r in res.results], axis=0)
```

```python
import contextlib
import numpy as np
import concourse.bass as bass
import concourse.mybir as mybir
from concourse.bass_utils import run_bass_kernel_spmd

F32 = mybir.dt.float32
BF16 = mybir.dt.bfloat16
AF = mybir.ActivationFunctionType
ALU = mybir.AluOpType

D = 2048
SEQ = 2048
NB = 16
IN_DIM = 25648
EPS = 1e-5
DEPTH = 2
ALPHA = (2 * DEPTH) ** 0.25
SEM_LIMIT = 30000
NSLOT = 4

SEC = {"pu": 0, "pz": 2048, "sx": 4096, "sz": 6144, "sB": 8192, "sC": 8704, "mq": 9248, "mk": 11296,
       "mv": 13344, "mo": 15392, "mz": 17440, "g": 19504}
SECN = {"pu": 8, "pz": 8, "sx": 8, "sz": 8, "sB": 2, "sC": 2, "mq": 8, "mk": 8, "mv": 8, "mo": 8, "mz": 8, "g": 24}
BLK0 = {}
_acc = 0
for _k in SEC:
    BLK0[_k] = _acc
    _acc += SECN[_k]
NWBLK = _acc


class Sem:
    __slots__ = ("h", "count", "owner")

    def __init__(self, h, owner):
        self.h = h
        self.count = 0
        self.owner = owner


class Buf:
    __slots__ = ("ap", "w", "r", "excl", "name")

    def __init__(self, ap, name="", excl=False):
        self.ap = ap
        self.w = None
        self.r = {}
        self.excl = excl
        self.name = name

    def __getitem__(self, k):
        return self.ap[k]


import sys as _sys


def _caller_line():
    f = _sys._getframe(2)
    while f is not None and f.f_code.co_name in ("emit", "pe", "act", "dve", "dma", "<lambda>", "proj_fm", "proj_tm", "to_fm", "rstd_from", "sp_load", "load_w"):
        f = f.f_back
    return f.f_lineno if f is not None else 0


class _Rec:
    def __getattr__(self, name):
        def f(*a, **k):
            return (name, a, k)
        return f


REC = _Rec()


class Prog:
    STREAMS = ("pe", "act", "dve", "pool", "sp")

    def __init__(self, nc, stack):
        self.nc = nc
        self.stack = stack
        self.ops = {s: [] for s in self.STREAMS}
        self.known = {s: {} for s in self.STREAMS}
        self.nsem = 0
        self.engsem = {s: self.new_sem(s) for s in ("pe", "act", "dve", "pool")}
        self.pending = {s: False for s in self.STREAMS}
        self.marks = []
        self.tags = {s: [] for s in self.STREAMS}
        self.names = {s: [] for s in self.STREAMS}

    def new_sem(self, owner):
        h = self.stack.enter_context(self.nc.semaphore(f"s{self.nsem}_{owner}"))
        self.nsem += 1
        return Sem(h, owner)

    def sbuf(self, shape, dtype, name):
        return self.stack.enter_context(self.nc.sbuf_tensor(name, list(shape), dtype))

    def psum(self, shape, dtype, name):
        return self.stack.enter_context(self.nc.psum_tensor(name, list(shape), dtype))

    def emit(self, stream, fn, outs=(), ins=(), inc=True, dma_sem=None):
        waits = {}
        known = self.known[stream]

        def need(sem, val):
            if stream == "pe" and sem.owner == "pe":
                return
            if known.get(sem, 0) >= val:
                return
            if waits.get(sem, 0) < val:
                waits[sem] = val

        for b in ins:
            if b.w is not None:
                need(*b.w)
            if b.excl:
                for sem, val in b.r.items():
                    need(sem, val)
        for b in outs:
            if b.w is not None:
                need(*b.w)
            for sem, val in b.r.items():
                need(sem, val)
        for sem, val in waits.items():
            known[sem] = val
        if fn is None:
            self.ops[stream].append((list(waits.items()), None, None, 0))
            return
        if dma_sem is not None:
            dma_sem.count += 16
            ev = (dma_sem, dma_sem.count)
            amt = 16
        else:
            es = self.engsem[stream]
            if es.count >= SEM_LIMIT and not self.pending[stream]:
                es = self.new_sem(stream)
                self.engsem[stream] = es
            if inc:
                es.count += 1
                ev = (es, es.count)
                self.pending[stream] = False
            else:
                ev = (es, es.count + 1)
                self.pending[stream] = True
            amt = 1 if inc else 0
        for b in ins:
            if b.excl:
                b.w = ev
                b.r = {}
            elif b.r.get(ev[0], 0) < ev[1]:
                b.r[ev[0]] = ev[1]
        for b in outs:
            b.w = ev
            b.r = {}
        self.ops[stream].append((list(waits.items()), fn(REC), ev[0], amt))
        self.tags[stream].append(_caller_line())

    def pe(self, fn, outs, ins, inc=True):
        self.emit("pe", fn, outs, ins, inc)

    def act(self, fn, outs, ins):
        self.emit("act", fn, outs, ins)

    def dve(self, fn, outs, ins):
        self.emit("dve", fn, outs, ins)

    def dma(self, stream, out_ap, in_ap, outs, ins, sem):
        self.emit(stream, lambda e: e.dma_start(out=out_ap, in_=in_ap), outs, ins, dma_sem=sem)

    def mark(self, name):
        self.marks.append((name, {t: sum(1 for o in self.ops[t] if o[1] is not None) for t in self.STREAMS}))

    def barrier(self, streams=("pe", "act", "dve"), extra=()):
        evs = [(self.engsem[t], self.engsem[t].count) for t in streams] + list(extra)
        for t in streams:
            assert not self.pending[t]
            waits = []
            for sem, val in evs:
                if val == 0 or (t == "pe" and sem.owner == "pe"):
                    continue
                if self.known[t].get(sem, 0) >= val:
                    continue
                self.known[t][sem] = val
                waits.append((sem, val))
            self.ops[t].append((waits, None, None, 0))

    def wait_for(self, stream, bufs):
        self.emit(stream, None, outs=bufs, ins=())

    def finish(self):
        nc = self.nc
        ops = self.ops
        with nc.Block() as block:
            def run(stream, e):
                for waits, fn, sem, amt in ops[stream]:
                    for wsem, val in waits:
                        e.wait_ge(wsem.h, val)
                    if fn is None:
                        continue
                    ins = getattr(e, fn[0])(*fn[1], **fn[2])
                    self.names[stream].append(ins.ins.name)
                    if amt:
                        ins.then_inc(sem.h, amt)

            @block.tensor
            def _(e):
                run("pe", e)

            @block.scalar
            def _(e):
                run("act", e)

            @block.vector
            def _(e):
                run("dve", e)

            @block.gpsimd
            def _(e):
                run("pool", e)

            @block.sync
            def _(e):
                run("sp", e)


def bc(ap, axis, shape):
    return ap.unsqueeze(axis).to_broadcast(list(shape))


def build_nc(layers, T, seq=SEQ):
    NT = T // 128
    NTILES = seq // T
    NL = len(layers)
    nc = bass.Bass("TRN2", target_bir_lowering=False)
    x_d = nc.dram_tensor("x", [seq, D], F32, kind="ExternalInput").ap()
    win_d = nc.dram_tensor("w_in_r", [DEPTH * NWBLK, 128, 4096], F32, kind="ExternalInput").ap()
    ws_d = nc.dram_tensor("ws_r", [DEPTH, 128, 768], F32, kind="ExternalInput").ap()
    wb_d = nc.dram_tensor("wb_r", [DEPTH * 24, 128, 4096], F32, kind="ExternalInput").ap()
    wo_d = nc.dram_tensor("wo_r", [DEPTH * 8, 128, 4096], F32, kind="ExternalInput").ap()
    wp_d = nc.dram_tensor("wp_r", [DEPTH * 4, 128, 2048], F32, kind="ExternalInput").ap()
    cols_d = nc.dram_tensor("cols_l", [DEPTH, 128, 184], F32, kind="ExternalInput").ap()
    srow_d = nc.dram_tensor("srow_l", [DEPTH, 128, 112], F32, kind="ExternalInput").ap()
    rows_d = nc.dram_tensor("rows_l", [DEPTH * 4, 128, 2048], F32, kind="ExternalInput").ap()
    cst_d = nc.dram_tensor("consts", [128, 704], F32, kind="ExternalInput").ap()
    out_d = nc.dram_tensor("out", [seq, D], F32, kind="ExternalOutput").ap()
    sc_d = {"in": nc.dram_tensor("wsc_in", [DEPTH * NWBLK, 128, 4096], BF16).ap(),
            "b": nc.dram_tensor("wsc_b", [DEPTH * 24, 128, 4096], BF16).ap(),
            "o": nc.dram_tensor("wsc_o", [DEPTH * 8, 128, 4096], BF16).ap(),
            "p": nc.dram_tensor("wsc_p", [DEPTH * 4, 128, 2048], BF16).ap()}
    src_d = {"in": win_d, "b": wb_d, "o": wo_d, "p": wp_d}

    with contextlib.ExitStack() as st:
        P = Prog(nc, st)

        def SB(shape, dt, name):
            return Buf(P.sbuf(shape, dt, name)[:], name)

        cst = SB([128, 704], F32, "cst")
        identb = SB([128, 128], BF16, "identb")
        cols = [SB([128, 184], F32, f"cols{i}") for i in range(NL)]
        srow = [SB([128, 112], F32, f"srow{i}") for i in range(NL)]
        Sst = [SB([128, 2048], F32, f"Sst{i}") for i in range(NL)]
        Cst = [[SB([128, 2, 257], F32, f"Cst{i}_{h}") for h in range(8)] for i in range(NL)]
        ctail = [SB([128, 24, 3], F32, f"ctail{i}") for i in range(NL)]
        ptail = [SB([128, 16, 16], F32, f"ptail{i}") for i in range(NL)]
        rowbuf = [SB([128, 2048], F32, f"rowbuf{i}") for i in range(2)]
        xres = [SB([128, 2048], F32, f"xres{c}") for c in range(NT)]
        xT = SB([128, NB, T], BF16, "xT")
        Y = [SB([128, NB, T], BF16, f"Y{b}") for b in range(3)]
        wslot = [SB([128, 4096], BF16, f"wslot{i}") for i in range(NSLOT)]
        wsm = SB([128, 768], BF16, "wsm")
        smalls = [SB([128, 128], F32, f"smalls{c}") for c in range(NT)]
        sexp = [SB([128, 128], F32, f"sexp{c}") for c in range(NT)]
        ss = SB([128, 8], F32, "ss")
        st6 = SB([128, 4, 6], F32, "st6")
        mv = SB([128, 2], F32, "mv")
        wgt = [SB([128, 8], F32, f"wgt{c}") for c in range(NT)]
        ARENA = 25600
        arena = P.sbuf([128, ARENA], BF16, "arena")[:]
        aoff = [0]

        pending_out = []

        def phase_begin(wait_out=False):
            if wait_out and pending_out:
                P.barrier(extra=[(sm_, sm_.count) for sm_ in pending_out])
                del pending_out[:]
            else:
                P.barrier()
            aoff[0] = 0

        def CV(shape, dt, name):
            n = 1
            for d_ in shape[1:]:
                n *= d_
            nb = n * (2 if dt == F32 else 1)
            nb = (nb + 1) // 2 * 2
            o = aoff[0]
            aoff[0] += nb
            assert aoff[0] <= ARENA, (name, aoff[0])
            ap = arena[:, o:o + nb]
            if dt == F32:
                ap = ap.bitcast(F32)
            ap = ap[:, 0:n]
            if len(shape) == 3:
                ap = ap.rearrange("p (a b) -> p a b", b=shape[2])
            return Buf(ap, name)

        mmb = [Buf(P.psum([128, 512], F32, f"mm{i}")[:], f"mm{i}", excl=True) for i in range(2)]
        smb = Buf(P.psum([128, 512], F32, "smb")[:], "smb", excl=True)
        dps = Buf(P.psum([128, 1024], F32, "dps")[:], "dps", excl=True)
        qkb = Buf(P.psum([128, 512], F32, "qkb")[:], "qkb", excl=True)
        totb = Buf(P.psum([128, 512], F32, "totb")[:], "totb", excl=True)
        trb = Buf(P.psum([128, 1024], BF16, "trb")[:], "trb", excl=True)
        mm_i = [0]

        def mm_bank():
            mm_i[0] ^= 1
            return mmb[mm_i[0]]

        ident = cst.ap[:, 0:128]
        tri_le = cst.ap[:, 128:256]
        mask_gt = cst.ap[:, 256:384]
        ones = cst.ap[:, 384:512]
        negmask = cst.ap[:, 512:640]
        icnt = cst.ap[:, 640:704]

        sp_sems = {}

        def sp_sem_for(buf):
            if id(buf) not in sp_sems:
                sp_sems[id(buf)] = P.new_sem("dma")
            return sp_sems[id(buf)]

        def sp_load(dst_buf, dst_ap, src_ap):
            P.dma("sp", dst_ap, src_ap, [dst_buf], [], sp_sem_for(dst_buf))

        wsem = [P.new_sem("wdma") for _ in range(NSLOT)]
        wsem_sw = [P.new_sem("wdma_sw") for _ in range(NSLOT)]
        wsm_sem = P.new_sem("wsm")
        w_i = [0]

        sc_bufs = {}
        wbsem = [P.new_sem("wback") for _ in range(NSLOT)]

        def load_w(kind, idx, n=4096):
            i = w_i[0] % NSLOT
            w_i[0] += 1
            b = wslot[i]
            key = (kind, idx)
            if key not in sc_bufs:
                P.dma("pool", b.ap[:, 0:n], src_d[kind][idx], [b], [], wsem_sw[i])
                sb = Buf(sc_d[kind][idx], f"sc_{kind}{idx}")
                sc_bufs[key] = sb
                P.dma("sp", sb.ap, b.ap[:, 0:n], [sb], [b], wbsem[i])
            else:
                P.dma("sp", b.ap[:, 0:n], sc_bufs[key].ap, [b], [sc_bufs[key]], wsem[i])
            return b

        def wv(b, F):
            return b.ap[:, 0:16 * F].rearrange("p (k f) -> p k f", f=F)

        sp_load(cst, cst.ap, cst_d)
        for i, l in enumerate(layers):
            sp_load(cols[i], cols[i].ap, cols_d[l])
            sp_load(srow[i], srow[i].ap, srow_d[l])
        P.dve(lambda e: e.tensor_copy(out=identb.ap, in_=ident), [identb], [cst])
        for i in range(NL):
            P.act(lambda e, i=i: e.activation(out=srow[i].ap[:, 32:64], in_=srow[i].ap[:, 32:64], func=AF.Exp), [srow[i]], [srow[i]])
            P.dve(lambda e, i=i: e.tensor_scalar(out=srow[i].ap[:, 32:64], in0=srow[i].ap[:, 32:64], scalar1=-1.0, scalar2=None, op0=ALU.mult), [srow[i]], [srow[i]])
            P.dve(lambda e, i=i: e.memset(Sst[i].ap, 0.0), [Sst[i]], [])
            for h in range(8):
                P.dve(lambda e, i=i, h=h: e.memset(Cst[i][h].ap, 0.0), [Cst[i][h]], [])
            P.dve(lambda e, i=i: e.memset(ctail[i].ap, 0.0), [ctail[i]], [])
            P.dve(lambda e, i=i: e.memset(ptail[i].ap, 0.0), [ptail[i]], [])

        def to_fm(src_buf, src_ap_fn, nblk, dst_buf, dst_ap_fn, extra_ins=()):
            j0 = 0
            while j0 < nblk:
                n = min(8, nblk - j0)
                for j in range(n):
                    P.pe(lambda e, j=j, j0=j0: e.transpose(trb.ap[:, j * 128:(j + 1) * 128], src_ap_fn(j0 + j), identb.ap),
                         [trb], [src_buf, identb], inc=(j == n - 1))
                P.act(lambda e, j0=j0, n=n: e.activation(out=dst_ap_fn(j0, n), in_=trb.ap[:, 0:n * 128].rearrange("p (a b) -> p a b", b=128), func=AF.Copy),
                      [dst_buf], [trb])
                j0 += n

        def proj_fm(wbuf, F, f0, rhs_buf, evac):
            w3 = wv(wbuf, F)
            ps = mm_bank()
            for kc in range(NB):
                P.pe(lambda e, kc=kc: e.matmul(ps.ap[:, 0:T], lhsT=w3[:, kc, f0:f0 + 128], rhs=rhs_buf.ap[:, kc, :], start=(kc == 0), stop=(kc == NB - 1)),
                     [ps], [wbuf, rhs_buf], inc=(kc == NB - 1))
            evac(ps)

        def proj_tm(wbufs, F, nf, c, lhs_buf, evac):
            ps = mm_bank()
            for wi, wbuf in enumerate(wbufs):
                w3 = wv(wbuf, F)
                for kc in range(NB):
                    P.pe(lambda e, kc=kc, w3=w3, wi=wi: e.matmul(ps.ap[:, wi * nf:(wi + 1) * nf], lhsT=lhs_buf.ap[:, kc, c * 128:(c + 1) * 128], rhs=w3[:, kc, 0:nf], start=(kc == 0), stop=(kc == NB - 1)),
                         [ps], [wbuf, lhs_buf], inc=(kc == NB - 1))
            evac(ps)

        def rstd_from(var_ap, buf, scale=1.0):
            P.act(lambda e: e.activation(out=var_ap, in_=var_ap, func=AF.Ln, bias=EPS, scale=scale), [buf], [buf])
            P.act(lambda e: e.activation(out=var_ap, in_=var_ap, func=AF.Exp, scale=-0.5), [buf], [buf])

        def run_threads(*gens):
            gens = [g_ for g_ in gens if g_ is not None]
            while gens:
                for g_ in list(gens):
                    try:
                        next(g_)
                    except StopIteration:
                        gens.remove(g_)
                    assert not P.pending["pe"]

        def layer_tile(ti, i, l, first, last):
            t0 = ti * T
            WIN = lambda key, j: ("in", l * NWBLK + BLK0[key] + j)
            cl = cols[i]
            sr = srow[i]
            P.mark(f"t{ti}l{l}:0")
            phase_begin()
            xb = CV([128, 2048], BF16, "xb")
            for c in range(NT):
                if first:
                    sp_load(xres[c], xres[c].ap, x_d[t0 + c * 128:t0 + (c + 1) * 128, :])
                P.act(lambda e, c=c: e.activation(out=xb.ap, in_=xres[c].ap, func=AF.Copy), [xb], [xres[c]])
                to_fm(xb, lambda j: xb.ap[:, j * 128:(j + 1) * 128], NB, xT,
                      lambda j0, n, c=c: xT.ap[:, j0:j0 + n, c * 128:(c + 1) * 128])

            P.dma("pool", wsm.ap, ws_d[l], [wsm], [], wsm_sem)
            wsm3 = wsm.ap.rearrange("p (k f) -> p k f", f=48)
            for c in range(NT):
                sm = smalls[c]
                for kc in range(NB):
                    P.pe(lambda e, kc=kc, c=c: e.matmul(smb.ap[:, 0:48], lhsT=xT.ap[:, kc, c * 128:(c + 1) * 128], rhs=wsm3[:, kc, :], start=(kc == 0), stop=(kc == NB - 1)),
                         [smb], [wsm, xT], inc=(kc == NB - 1))
                P.dve(lambda e, sm=sm: e.tensor_tensor(out=sm.ap[:, 0:32], in0=smb.ap[:, 0:32], in1=sr.ap[:, 0:32], op=ALU.add), [sm], [smb, sr])
                P.dve(lambda e, sm=sm: e.tensor_tensor(out=sm.ap[:, 96:104], in0=smb.ap[:, 32:40], in1=sr.ap[:, 96:104], op=ALU.add), [sm], [smb, sr])
                P.dve(lambda e, sm=sm: e.tensor_tensor(out=sm.ap[:, 104:112], in0=smb.ap[:, 40:48], in1=sr.ap[:, 104:112], op=ALU.add), [sm], [smb, sr])
                P.act(lambda e, sm=sm: e.activation(out=sm.ap[:, 0:32], in_=sm.ap[:, 0:32], func=AF.Exp), [sm], [sm])
                P.act(lambda e, sm=sm: e.activation(out=sm.ap[:, 0:32], in_=sm.ap[:, 0:32], func=AF.Ln, bias=1.0, scale=1.0), [sm], [sm])
                P.act(lambda e, sm=sm: e.activation(out=sm.ap[:, 32:64], in_=sm.ap[:, 0:32], func=AF.Ln), [sm], [sm])
                P.dve(lambda e, sm=sm: e.tensor_tensor(out=sm.ap[:, 64:96], in0=sm.ap[:, 0:32], in1=sr.ap[:, 32:64], op=ALU.mult), [sm], [sm, sr])
                P.act(lambda e, sm=sm: e.activation(out=sm.ap[:, 104:112], in_=sm.ap[:, 104:112], func=AF.Exp, scale=-1.0), [sm], [sm])
                P.act(lambda e, sm=sm: e.activation(out=sm.ap[:, 104:112], in_=sm.ap[:, 104:112], func=AF.Ln, bias=1.0, scale=1.0), [sm], [sm])
                P.dve(lambda e, sm=sm: e.tensor_scalar(out=sm.ap[:, 104:112], in0=sm.ap[:, 104:112], scalar1=-1.0, scalar2=None, op0=ALU.mult), [sm], [sm])
                a_ap = sm.ap[:, 64:96]
                P.pe(lambda e, a_ap=a_ap: e.matmul(smb.ap[:, 0:32], lhsT=tri_le, rhs=a_ap, start=True, stop=True), [smb], [cst, sm], inc=False)
                P.pe(lambda e, a_ap=a_ap: e.matmul(smb.ap[:, 32:64], lhsT=mask_gt, rhs=a_ap, start=True, stop=True), [smb], [cst, sm], inc=False)
                P.pe(lambda e, a_ap=a_ap: e.matmul(smb.ap[:, 64:96], lhsT=ones, rhs=a_ap, start=True, stop=True), [smb], [cst, sm], inc=True)
                P.act(lambda e, c=c: e.activation(out=sexp[c].ap[:, 0:96], in_=smb.ap[:, 0:96], func=AF.Exp), [sexp[c]], [smb])
                P.dve(lambda e, c=c, sm=sm: e.tensor_tensor(out=sexp[c].ap[:, 96:128], in0=sexp[c].ap[:, 32:64], in1=sm.ap[:, 0:32], op=ALU.mult), [sexp[c]], [sexp[c], sm])

            P.mark(f"t{ti}l{l}:A")
            phase_begin(wait_out=True)
            pA = CV([128, 16 + T], F32, "pA")
            pB = CV([128, 16 + T], F32, "pB")
            pC = CV([128, 16 + T], F32, "pC")
            p16 = CV([128, 16], F32, "p16")
            pooled = CV([128, 4, T], BF16, "pooled")
            spz = CV([128, 4, T], BF16, "spz")
            for g in range(4):
                nsteps = g + 1
                wwin = 2 ** (g + 1)
                for jb in range(2):
                    wu = load_w(*WIN("pu", g * 2 + jb))
                    wz = load_w(*WIN("pz", g * 2 + jb))
                    for m in range(2):
                        q = jb * 2 + m
                        blk = g * 4 + q

                        def evac_pu(ps, q=q, blk=blk):
                            if ti == 0:
                                P.dve(lambda e: e.memset(pA.ap[:, 0:16], 0.0), [pA], [])
                            else:
                                P.dve(lambda e: e.tensor_copy(out=pA.ap[:, 0:16], in_=ptail[i].ap[:, blk, :]), [pA], [ptail[i]])
                            P.act(lambda e: e.activation(out=pA.ap[:, 16:16 + T], in_=ps.ap[:, 0:T], func=AF.Copy), [pA], [ps])
                            P.dve(lambda e: e.tensor_copy(out=ptail[i].ap[:, blk, :], in_=pA.ap[:, T:T + 16]), [ptail[i]], [pA])
                            cur = pA
                            lo = 0
                            for k in range(nsteps):
                                stp = 2 ** k
                                lo += stp
                                dst = pB if k % 2 == 0 else pC
                                P.dve(lambda e, cur=cur, dst=dst, lo=lo, stp=stp: e.tensor_tensor(out=dst.ap[:, lo:16 + T], in0=cur.ap[:, lo:16 + T], in1=cur.ap[:, lo - stp:16 + T - stp], op=ALU.add),
                                      [dst], [cur])
                                cur = dst
                            P.dve(lambda e, cur=cur: e.scalar_tensor_tensor(out=pooled.ap[:, q, :], in0=cur.ap[:, 16:16 + T], scalar=1.0 / wwin, in1=pA.ap[:, 16:16 + T], op0=ALU.mult, op1=ALU.subtract),
                                  [pooled], [cur, pA])
                            if ti == 0:
                                P.dve(lambda e, cur=cur: e.tensor_tensor(out=p16.ap, in0=cur.ap[:, 16:32], in1=icnt[:, g * 16:(g + 1) * 16], op=ALU.mult), [p16], [cur, cst])
                                P.dve(lambda e: e.tensor_tensor(out=pooled.ap[:, q, 0:16], in0=p16.ap, in1=pA.ap[:, 16:32], op=ALU.subtract), [pooled], [p16, pA])

                        proj_fm(wu, 256, m * 128, xT, evac_pu)
                        proj_fm(wz, 256, m * 128, xT,
                                lambda ps, q=q: P.act(lambda e: e.activation(out=spz.ap[:, q, :], in_=ps.ap[:, 0:T], func=AF.Silu), [spz], [ps]))
                wp = load_w("p", l * 4 + g, 2048)
                wp3 = wp.ap[:, 0:2048].rearrange("p (k f) -> p k f", f=512)
                for m in range(4):
                    ps = mm_bank()
                    for kc in range(4):
                        P.pe(lambda e, kc=kc, m=m, ps=ps: e.matmul(ps.ap[:, 0:T], lhsT=wp3[:, kc, m * 128:(m + 1) * 128], rhs=pooled.ap[:, kc, :], start=(kc == 0), stop=(kc == 3)),
                             [ps], [wp, pooled], inc=(kc == 3))
                    P.dve(lambda e, m=m, ps=ps: e.scalar_tensor_tensor(out=Y[0].ap[:, g * 4 + m, :], in0=ps.ap[:, 0:T], scalar=cl.ap[:, 48 + g * 4 + m:48 + g * 4 + m + 1], in1=spz.ap[:, m, :], op0=ALU.mult, op1=ALU.mult),
                          [Y[0]], [ps, cl, spz])

            P.mark(f"t{ti}l{l}:B")
            phase_begin()
            craw2 = [CV([128, 4 + T], F32, f"craw{k_}") for k_ in range(2)]
            cacc2 = [CV([128, T], F32, f"cacc{k_}") for k_ in range(2)]
            cv_i = [0]
            cv_pend = []

            def conv_flush():
                while cv_pend:
                    cb_, oap_, ob_ = cv_pend.pop(0)
                    P.act(lambda e: e.activation(out=oap_, in_=cb_.ap, func=AF.Silu), [ob_], [cb_])
            xfm = CV([128, 4, T], BF16, "xfm")
            BT = [CV([128, T], BF16, f"BT{k_}") for k_ in range(4)]
            CT = [CV([128, T], BF16, f"CT{k_}") for k_ in range(4)]
            Xtok2 = [CV([128, NT, 512], BF16, f"Xtok{k_}") for k_ in range(2)]
            Btok2 = [CV([128, NT, 128], BF16, f"Btok{k_}") for k_ in range(2)]
            szg2 = [CV([128, NT, 512], F32, f"szg{k_}") for k_ in range(2)]
            LT = CV([128, 8, 128], F32, "LT")
            ADDt = CV([128, 8, 128], F32, "ADDt")
            MT4 = [CV([128, 8, 128], BF16, f"MT{k_}") for k_ in range(2 * NT)]
            Sbf = CV([128, 512], BF16, "Sbf")
            y1 = CV([128, 512], F32, "y1")
            y2 = CV([128, 512], F32, "y2")
            y3 = CV([128, 512], F32, "y3")
            ytok = CV([128, 512], BF16, "ytok")
            Xd = CV([128, 512], BF16, "Xd")
            sp_load(rowbuf[0], rowbuf[0].ap, rows_d[l * 4 + 0])

            def conv_block(ps, cb, out_ap, out_buf):
                cv_i[0] ^= 1
                craw, cacc = craw2[cv_i[0]], cacc2[cv_i[0]]
                P.dve(lambda e: e.tensor_copy(out=craw.ap[:, 0:3], in_=ctail[i].ap[:, cb, :]), [craw], [ctail[i]])
                P.act(lambda e: e.activation(out=craw.ap[:, 3:3 + T], in_=ps.ap[:, 0:T], func=AF.Copy), [craw], [ps])
                P.dve(lambda e: e.tensor_copy(out=ctail[i].ap[:, cb, :], in_=craw.ap[:, T:T + 3]), [ctail[i]], [craw])
                wcol = lambda j: cl.ap[:, 64 + j * 24 + cb:64 + j * 24 + cb + 1]
                bcol = cl.ap[:, 160 + cb:160 + cb + 1]
                P.dve(lambda e: e.tensor_scalar(out=cacc.ap, in0=craw.ap[:, 3:3 + T], scalar1=wcol(3), scalar2=bcol, op0=ALU.mult, op1=ALU.add), [cacc], [craw, cl])
                for j in range(3):
                    P.dve(lambda e, j=j: e.scalar_tensor_tensor(out=cacc.ap, in0=craw.ap[:, j:j + T], scalar=wcol(j), in1=cacc.ap, op0=ALU.mult, op1=ALU.add), [cacc], [craw, cl, cacc])
                conv_flush()
                cv_pend.append((cacc, out_ap, out_buf))

            v3 = lambda ap: ap.rearrange("p (a b) -> p a b", b=64)

            def gen_proj_BC(g):
                if g < 4 and g % 2 == 0:
                    wB = load_w(*WIN("sB", g // 2))
                    wC = load_w(*WIN("sC", g // 2))
                    for m in range(2):
                        proj_fm(wB, 256, m * 128, xT, lambda ps, m=m: conv_block(ps, 16 + g + m, BT[g + m].ap, BT[g + m]))
                        yield
                        proj_fm(wC, 256, m * 128, xT, lambda ps, m=m: conv_block(ps, 20 + g + m, CT[g + m].ap, CT[g + m]))
                        if m == 1:
                            conv_flush()
                        yield

            def gen_proj_X(g):
                pb = g % 2
                Xtok, Btok, szg = Xtok2[pb], Btok2[pb], szg2[pb]
                for jb in range(2):
                    wx = load_w(*WIN("sx", g * 2 + jb))
                    for m in range(2):
                        q = jb * 2 + m
                        proj_fm(wx, 256, m * 128, xT, lambda ps, q=q: conv_block(ps, g * 4 + q, xfm.ap[:, q, :], xfm))
                        if q == 3:
                            conv_flush()
                        yield
                for c in range(NT):
                    to_fm(xfm, lambda j, c=c: xfm.ap[:, j, c * 128:(c + 1) * 128], 4, Xtok,
                          lambda j0, n, c=c: Xtok.ap[:, c, :].rearrange("p (a b) -> p a b", b=128))
                    to_fm(BT[g], lambda j, c=c: BT[g].ap[:, c * 128:(c + 1) * 128], 1, Btok,
                          lambda j0, n, c=c: Btok.ap[:, c, :].rearrange("p (a b) -> p a b", b=128))
                    yield
                wz0 = load_w(*WIN("sz", g * 2))
                wz1 = load_w(*WIN("sz", g * 2 + 1))
                for c in range(NT):
                    proj_tm([wz0, wz1], 256, 256, c, xT,
                            lambda ps, c=c: P.act(lambda e: e.activation(out=szg.ap[:, c, :], in_=ps.ap[:, 0:512], func=AF.Silu), [szg], [ps]))
                    yield

            def gen_front_B(g):
                BTg, CTg = BT[g], CT[g]
                for c in range(NT):
                    MT = MT4[(g % 2) * NT + c]
                    sm = smalls[c]
                    a_g = sm.ap[:, 64 + g * 8:64 + (g + 1) * 8]
                    lndt_g = sm.ap[:, 32 + g * 8:32 + (g + 1) * 8]
                    cs = slice(c * 128, (c + 1) * 128)
                    P.dve(lambda e: e.tensor_tensor(out=LT.ap, in0=bc(tri_le, 1, [128, 8, 128]), in1=bc(a_g, 2, [128, 8, 128]), op=ALU.mult), [LT], [cst, sm])
                    P.dve(lambda e: e.tensor_tensor(out=ADDt.ap, in0=bc(negmask, 1, [128, 8, 128]), in1=bc(lndt_g, 2, [128, 8, 128]), op=ALU.add), [ADDt], [cst, sm])
                    LT2 = LT.ap.rearrange("p a b -> p (a b)")
                    AD2 = ADDt.ap.rearrange("p a b -> p (a b)")
                    for hh in range(2):
                        P.pe(lambda e: e.matmul(dps.ap[:, hh * 512:(hh + 1) * 512], lhsT=mask_gt, rhs=LT2[:, hh * 512:(hh + 1) * 512], start=True, stop=False), [dps], [cst, LT], inc=False)
                        P.pe(lambda e: e.matmul(dps.ap[:, hh * 512:(hh + 1) * 512], lhsT=ident, rhs=AD2[:, hh * 512:(hh + 1) * 512], start=False, stop=True), [dps], [cst, ADDt], inc=(hh == 1))
                    P.pe(lambda e: e.matmul(qkb.ap[:, 0:128], lhsT=BTg.ap[:, cs], rhs=CTg.ap[:, cs], start=True, stop=True), [qkb], [BTg, CTg])
                    yield
                    P.act(lambda e: e.activation(out=LT2, in_=dps.ap, func=AF.Exp), [LT], [dps])
                    P.dve(lambda e: e.tensor_tensor(out=MT.ap, in0=LT.ap, in1=bc(qkb.ap[:, 0:128], 1, [128, 8, 128]), op=ALU.mult), [MT], [LT, qkb])
                    yield

            def gen_back_B(g):
                pb = g % 2
                Xtok, Btok, szg = Xtok2[pb], Btok2[pb], szg2[pb]
                CTg = CT[g]
                Sg = Sst[i].ap[:, g * 512:(g + 1) * 512]
                dsk = sr.ap[:, 64 + g * 8:64 + (g + 1) * 8]

                def u1(c):
                    MT = MT4[(g % 2) * NT + c]
                    dd_g = sexp[c].ap[:, 96 + g * 8:96 + (g + 1) * 8]
                    cs = slice(c * 128, (c + 1) * 128)
                    P.dve(lambda e: e.tensor_copy(out=Sbf.ap, in_=Sg), [Sbf], [Sst[i]])
                    P.pe(lambda e: e.matmul(smb.ap[:, 0:512], lhsT=CTg.ap[:, cs], rhs=Sbf.ap, start=True, stop=True), [smb], [CTg, Sbf])
                    for h in range(8):
                        P.pe(lambda e: e.matmul(totb.ap[:, h * 64:(h + 1) * 64], lhsT=MT.ap[:, h, :], rhs=Xtok.ap[:, c, h * 64:(h + 1) * 64], start=True, stop=True), [totb], [MT, Xtok], inc=(h == 7))
                    P.dve(lambda e: e.tensor_tensor(out=v3(Xd.ap), in0=v3(Xtok.ap[:, c, :]), in1=bc(dd_g, 2, [128, 8, 64]), op=ALU.mult), [Xd], [Xtok, sexp[c]])

                def u2(c):
                    se = sexp[c]
                    eA_g = se.ap[:, g * 8:(g + 1) * 8]
                    cd_g = se.ap[:, 64 + g * 8:64 + (g + 1) * 8]
                    P.dve(lambda e: e.tensor_tensor(out=v3(y1.ap), in0=v3(smb.ap[:, 0:512]), in1=bc(eA_g, 2, [128, 8, 64]), op=ALU.mult), [y1], [smb, se])
                    P.pe(lambda e: e.matmul(smb.ap[:, 0:512], lhsT=Btok.ap[:, c, :], rhs=Xd.ap, start=True, stop=True), [smb], [Btok, Xd])
                    P.dve(lambda e: e.tensor_tensor(out=y2.ap, in0=totb.ap[:, 0:512], in1=y1.ap, op=ALU.add), [y2], [totb, y1])
                    P.dve(lambda e: e.tensor_tensor(out=v3(y1.ap), in0=v3(Xtok.ap[:, c, :]), in1=bc(dsk, 2, [128, 8, 64]), op=ALU.mult), [y1], [Xtok, sr])
                    P.dve(lambda e: e.tensor_tensor(out=y2.ap, in0=y2.ap, in1=y1.ap, op=ALU.add), [y2], [y2, y1])
                    P.dve(lambda e: e.tensor_tensor(out=y2.ap, in0=y2.ap, in1=szg.ap[:, c, :], op=ALU.mult), [y2], [y2, szg])
                    P.act(lambda e: e.activation(out=y1.ap, in_=y2.ap, func=AF.Square, accum_out=ss.ap[:, 0:1]), [y1, ss], [y2])
                    P.dve(lambda e: e.tensor_tensor(out=v3(y3.ap), in0=v3(Sg), in1=bc(cd_g, 2, [128, 8, 64]), op=ALU.mult), [y3], [Sst[i], se])
                    P.dve(lambda e: e.tensor_tensor(out=Sg, in0=smb.ap[:, 0:512], in1=y3.ap, op=ALU.add), [Sst[i]], [smb, y3])

                def u3(c):
                    rstd_from(ss.ap[:, 0:1], ss, scale=1.0 / 512)
                    P.dve(lambda e: e.scalar_tensor_tensor(out=ytok.ap, in0=y2.ap, scalar=ss.ap[:, 0:1], in1=rowbuf[0].ap[:, g * 512:(g + 1) * 512], op0=ALU.mult, op1=ALU.mult), [ytok], [y2, ss, rowbuf[0]])
                    to_fm(ytok, lambda j: ytok.ap[:, j * 128:(j + 1) * 128], 4, Y[1],
                          lambda j0, n: Y[1].ap[:, g * 4:g * 4 + 4, c * 128:(c + 1) * 128])

                u1(0)
                yield
                for c in range(NT):
                    u2(c)
                    yield
                    if c + 1 < NT:
                        u1(c + 1)
                        yield
                    u3(c)
                    yield

            def g_seq(*gens):
                for g_ in gens:
                    if g_ is not None:
                        yield from g_

            def g_par(*gens):
                gens = [g_ for g_ in gens if g_ is not None]
                while gens:
                    for g_ in list(gens):
                        try:
                            next(g_)
                            yield
                        except StopIteration:
                            gens.remove(g_)

            run_threads(gen_proj_BC(0))
            run_threads(gen_front_B(0), gen_proj_X(0))
            for g in range(4):
                if g < 3:
                    run_threads(gen_back_B(g), g_seq(gen_proj_BC(g + 1), g_par(gen_front_B(g + 1), gen_proj_X(g + 1))))
                else:
                    run_threads(gen_back_B(g))

            P.mark(f"t{ti}l{l}:C")
            phase_begin()
            EB = [CV([128, 8, 128], F32, f"EB{c}") for c in range(NT)]
            EE = [CV([128, 8, 128], F32, f"EE{c}") for c in range(NT)]
            PT = CV([128, 128], BF16, "PT")
            qp = CV([128, 2, 128], BF16, "qp")
            Cbf = CV([128, 2, 258], BF16, "Cbf")
            hc = CV([128, 256], F32, "hc")
            hy = CV([128, 256], F32, "hy")
            hyb = CV([128, 256], BF16, "hyb")

            def cset(k_):
                return dict(qT=CV([128, 2, T], BF16, f"qT{k_}"), kT=CV([128, 2, T], BF16, f"kT{k_}"),
                            vaug=CV([128, NT, 258], BF16, f"vaug{k_}"), sigo=CV([128, NT, 256], F32, f"sigo{k_}"),
                            silz=CV([128, NT, 256], F32, f"silz{k_}"), kw=CV([128, NT, 256], BF16, f"kw{k_}"))

            cs0 = cset(0)
            ov = aoff[0]
            LT = CV([128, 8, 128], F32, "LT")
            ADDt = CV([128, 8, 128], F32, "ADDt")
            sp_load(rowbuf[1], rowbuf[1].ap, rows_d[l * 4 + 1])
            for c in range(NT):
                sm = smalls[c]
                ig = sm.ap[:, 96:104]
                lf = sm.ap[:, 104:112]
                P.pe(lambda e: e.matmul(smb.ap[:, 0:8], lhsT=mask_gt, rhs=lf, start=True, stop=False), [smb], [cst, sm], inc=False)
                P.pe(lambda e: e.matmul(smb.ap[:, 0:8], lhsT=ident, rhs=ig, start=False, stop=True), [smb], [cst, sm])
                P.act(lambda e: e.activation(out=wgt[c].ap, in_=smb.ap[:, 0:8], func=AF.Exp), [wgt[c]], [smb])
                P.dve(lambda e: e.tensor_tensor(out=LT.ap, in0=bc(tri_le, 1, [128, 8, 128]), in1=bc(lf, 2, [128, 8, 128]), op=ALU.mult), [LT], [cst, sm])
                P.dve(lambda e: e.tensor_tensor(out=ADDt.ap, in0=bc(negmask, 1, [128, 8, 128]), in1=bc(ig, 2, [128, 8, 128]), op=ALU.add), [ADDt], [cst, sm])
                LT2 = LT.ap.rearrange("p a b -> p (a b)")
                AD2 = ADDt.ap.rearrange("p a b -> p (a b)")
                for hh in range(2):
                    P.pe(lambda e: e.matmul(dps.ap[:, hh * 512:(hh + 1) * 512], lhsT=ones, rhs=LT2[:, hh * 512:(hh + 1) * 512], start=True, stop=True), [dps], [cst, LT], inc=(hh == 1))
                P.act(lambda e: e.activation(out=EB[c].ap.rearrange("p a b -> p (a b)"), in_=dps.ap, func=AF.Exp), [EB[c]], [dps])
                for hh in range(2):
                    P.pe(lambda e: e.matmul(dps.ap[:, hh * 512:(hh + 1) * 512], lhsT=mask_gt, rhs=LT2[:, hh * 512:(hh + 1) * 512], start=True, stop=False), [dps], [cst, LT], inc=False)
                    P.pe(lambda e: e.matmul(dps.ap[:, hh * 512:(hh + 1) * 512], lhsT=ident, rhs=AD2[:, hh * 512:(hh + 1) * 512], start=False, stop=True), [dps], [cst, ADDt], inc=(hh == 1))
                P.act(lambda e: e.activation(out=EE[c].ap.rearrange("p a b -> p (a b)"), in_=dps.ap, func=AF.Exp), [EE[c]], [dps])
            P.barrier()
            aoff[0] = ov
            cs1 = cset(1)
            csets = [cs0, cs1]
            for k_ in range(2):
                P.dve(lambda e: e.memset(csets[k_]["vaug"].ap[:, :, 256:257], 1.0), [csets[k_]["vaug"]], [])

            def gen_proj_C(h):
                S_ = csets[h % 2]
                qT, kT, vaug, sigo, silz, kw = S_["qT"], S_["kT"], S_["vaug"], S_["sigo"], S_["silz"], S_["kw"]
                wq = load_w(*WIN("mq", h))
                wk = load_w(*WIN("mk", h))
                for m in range(2):
                    proj_fm(wq, 256, m * 128, xT, lambda ps, m=m: P.act(lambda e: e.activation(out=qT.ap[:, m, :], in_=ps.ap[:, 0:T], func=AF.Copy), [qT], [ps]))
                    yield
                    proj_fm(wk, 256, m * 128, xT, lambda ps, m=m: P.act(lambda e: e.activation(out=kT.ap[:, m, :], in_=ps.ap[:, 0:T], func=AF.Copy, scale=0.0625), [kT], [ps]))
                    yield
                wvv = load_w(*WIN("mv", h))
                wo_ = load_w(*WIN("mo", h))
                for c in range(NT):
                    proj_tm([wvv], 256, 256, c, xT, lambda ps, c=c: P.act(lambda e: e.activation(out=vaug.ap[:, c, 0:256], in_=ps.ap[:, 0:256], func=AF.Copy), [vaug], [ps]))
                    yield
                    proj_tm([wo_], 256, 256, c, xT, lambda ps, c=c: P.act(lambda e: e.activation(out=sigo.ap[:, c, :], in_=ps.ap[:, 0:256], func=AF.Sigmoid), [sigo], [ps]))
                    yield
                wzz = load_w(*WIN("mz", h))
                for c in range(NT):
                    proj_tm([wzz], 256, 256, c, xT, lambda ps, c=c: P.act(lambda e: e.activation(out=silz.ap[:, c, :], in_=ps.ap[:, 0:256], func=AF.Silu), [silz], [ps]))
                    P.dve(lambda e: e.tensor_tensor(out=silz.ap[:, c, :], in0=silz.ap[:, c, :], in1=rowbuf[1].ap[:, h * 256:(h + 1) * 256], op=ALU.mult), [silz], [silz, rowbuf[1]])
                    yield
                for c in range(NT):
                    cs = slice(c * 128, (c + 1) * 128)
                    for m in range(2):
                        P.pe(lambda e: e.transpose(trb.ap[:, m * 128:(m + 1) * 128], kT.ap[:, m, cs], identb.ap), [trb], [kT, identb], inc=(m == 1))
                    P.dve(lambda e: e.tensor_scalar(out=kw.ap[:, c, :], in0=trb.ap[:, 0:256], scalar1=wgt[c].ap[:, h:h + 1], scalar2=None, op0=ALU.mult), [kw], [trb, wgt[c]])
                    yield

            def gen_core_C(h):
                S_ = csets[h % 2]
                qT, kT, vaug, sigo, silz, kw = S_["qT"], S_["kT"], S_["vaug"], S_["sigo"], S_["silz"], S_["kw"]
                for c in range(NT):
                    cs = slice(c * 128, (c + 1) * 128)
                    for m in range(2):
                        P.pe(lambda e: e.matmul(qkb.ap[:, 0:128], lhsT=kT.ap[:, m, cs], rhs=qT.ap[:, m, cs], start=(m == 0), stop=(m == 1)), [qkb], [kT, qT], inc=(m == 1))
                    P.dve(lambda e: e.tensor_tensor(out=qp.ap, in0=qT.ap[:, :, cs], in1=bc(EB[c].ap[:, h, :], 1, [128, 2, 128]), op=ALU.mult), [qp], [qT, EB[c]])
                    P.dve(lambda e: e.tensor_copy(out=Cbf.ap[:, :, 0:257], in_=Cst[i][h].ap), [Cbf], [Cst[i][h]])
                    yield
                    P.dve(lambda e: e.tensor_tensor(out=PT.ap, in0=qkb.ap[:, 0:128], in1=EE[c].ap[:, h, :], op=ALU.mult), [PT], [qkb, EE[c]])
                    P.pe(lambda e: e.matmul(totb.ap[:, 0:257], lhsT=PT.ap, rhs=vaug.ap[:, c, 0:257], start=True, stop=False), [totb], [PT, vaug], inc=False)
                    for m in range(2):
                        P.pe(lambda e: e.matmul(totb.ap[:, 0:257], lhsT=qp.ap[:, m, :], rhs=Cbf.ap[:, m, 0:257], start=False, stop=(m == 1)), [totb], [qp, Cbf], inc=(m == 1))
                    for m in range(2):
                        P.pe(lambda e: e.matmul(dps.ap[:, m * 512:m * 512 + 257], lhsT=kw.ap[:, c, m * 128:(m + 1) * 128], rhs=vaug.ap[:, c, 0:257], start=True, stop=True), [dps], [kw, vaug], inc=(m == 1))
                    yield
                    P.dve(lambda e: e.tensor_tensor(out=hc.ap, in0=totb.ap[:, 0:256], in1=sigo.ap[:, c, :], op=ALU.mult), [hc], [totb, sigo])
                    P.act(lambda e: e.activation(out=ss.ap[:, 1:2], in_=totb.ap[:, 256:257], func=AF.Square), [ss], [totb])
                    P.dve(lambda e: e.bn_stats(out=st6.ap[:, 0, :], in_=hc.ap), [st6], [hc])
                    P.dve(lambda e: e.bn_aggr(out=mv.ap, in_=st6.ap[:, 0, :]), [mv], [st6])
                    P.dve(lambda e: e.tensor_scalar(out=ss.ap[:, 1:2], in0=ss.ap[:, 1:2], scalar1=1.0, scalar2=EPS, op0=ALU.max, op1=ALU.mult), [ss], [ss])
                    P.act(lambda e: e.activation(out=mv.ap[:, 1:2], in_=mv.ap[:, 1:2], func=AF.Ln, bias=ss.ap[:, 1:2], scale=1.0), [mv], [mv, ss])
                    for m in range(2):
                        P.dve(lambda e: e.scalar_tensor_tensor(out=Cst[i][h].ap[:, m, :], in0=Cst[i][h].ap[:, m, :], scalar=EB[c].ap[:, h, 127:128], in1=dps.ap[:, m * 512:m * 512 + 257], op0=ALU.mult, op1=ALU.add),
                              [Cst[i][h]], [Cst[i][h], EB[c], dps])
                    P.act(lambda e: e.activation(out=mv.ap[:, 1:2], in_=mv.ap[:, 1:2], func=AF.Exp, scale=-0.5), [mv], [mv])
                    P.dve(lambda e: e.tensor_scalar(out=hy.ap, in0=hc.ap, scalar1=mv.ap[:, 0:1], scalar2=mv.ap[:, 1:2], op0=ALU.subtract, op1=ALU.mult), [hy], [hc, mv])
                    P.dve(lambda e: e.tensor_tensor(out=hyb.ap, in0=hy.ap, in1=silz.ap[:, c, :], op=ALU.mult), [hyb], [hy, silz])
                    to_fm(hyb, lambda j: hyb.ap[:, j * 128:(j + 1) * 128], 2, Y[2],
                          lambda j0, n: Y[2].ap[:, h * 2:h * 2 + 2, c * 128:(c + 1) * 128])
                    yield

            run_threads(gen_proj_C(0))
            for h in range(8):
                run_threads(gen_core_C(h), gen_proj_C(h + 1) if h < 7 else None)

            P.mark(f"t{ti}l{l}:D")
            phase_begin()
            _rsv = CV([128, 4096], BF16, "rsv")
            ostage = [CV([128, 2048], F32, f"ostage{c}") for c in range(NT)] if last else None
            mergedT = CV([128, NB, T], BF16, "mergedT")
            gsb = CV([128, T], F32, "gsb")
            gtmp = CV([128, T], F32, "gtmp")
            macc = [CV([128, T], F32, f"macc{m}") for m in range(2)]
            sp_load(rowbuf[0], rowbuf[0].ap, rows_d[l * 4 + 2])
            sp_load(rowbuf[1], rowbuf[1].ap, rows_d[l * 4 + 3])
            for j8 in range(8):
                for b in range(3):
                    wg = load_w(*WIN("g", b * 8 + j8))
                    wbb = load_w("b", (l * 3 + b) * 8 + j8)
                    for m in range(2):
                        j = j8 * 2 + m
                        proj_fm(wg, 256, m * 128, xT,
                                lambda ps, j=j, b=b: P.act(lambda e: e.activation(out=gsb.ap, in_=ps.ap[:, 0:T], func=AF.Sigmoid, bias=cl.ap[:, b * 16 + j:b * 16 + j + 1], scale=1.0), [gsb], [ps, cl]))

                        def evac_p(ps, j=j, b=b, m=m):
                            if b == 0:
                                P.dve(lambda e: e.tensor_tensor(out=macc[m].ap, in0=ps.ap[:, 0:T], in1=gsb.ap, op=ALU.mult), [macc[m]], [ps, gsb])
                            else:
                                P.dve(lambda e: e.tensor_tensor(out=gtmp.ap, in0=ps.ap[:, 0:T], in1=gsb.ap, op=ALU.mult), [gtmp], [ps, gsb])
                                if b == 1:
                                    P.dve(lambda e: e.tensor_tensor(out=macc[m].ap, in0=macc[m].ap, in1=gtmp.ap, op=ALU.add), [macc[m]], [macc[m], gtmp])
                                else:
                                    P.dve(lambda e: e.tensor_tensor(out=mergedT.ap[:, j, :], in0=macc[m].ap, in1=gtmp.ap, op=ALU.add), [mergedT], [macc[m], gtmp])

                        proj_fm(wbb, 256, m * 128, Y[b], evac_p)
            for n8 in range(8):
                wo = load_w("o", l * 8 + n8)
                for c in range(NT):
                    proj_tm([wo], 256, 256, c, mergedT,
                            lambda ps, c=c, n8=n8: P.dve(lambda e: e.scalar_tensor_tensor(out=xres[c].ap[:, n8 * 256:(n8 + 1) * 256], in0=xres[c].ap[:, n8 * 256:(n8 + 1) * 256], scalar=ALPHA, in1=ps.ap[:, 0:256], op0=ALU.mult, op1=ALU.add),
                                                         [xres[c]], [xres[c], ps]))
            for c in range(NT):
                xr = xres[c]
                for s4 in range(4):
                    P.dve(lambda e, s4=s4, xr=xr: e.bn_stats(out=st6.ap[:, s4, :], in_=xr.ap[:, s4 * 512:(s4 + 1) * 512]), [st6], [xr])
                P.dve(lambda e: e.bn_aggr(out=mv.ap, in_=st6.ap.rearrange("p a b -> p (a b)")), [mv], [st6])
                rstd_from(mv.ap[:, 1:2], mv)
                P.dve(lambda e, xr=xr: e.tensor_scalar(out=xr.ap, in0=xr.ap, scalar1=mv.ap[:, 0:1], scalar2=mv.ap[:, 1:2], op0=ALU.subtract, op1=ALU.mult), [xr], [xr, mv])
                P.dve(lambda e, xr=xr: e.tensor_tensor(out=xr.ap, in0=xr.ap, in1=rowbuf[0].ap, op=ALU.mult), [xr], [xr, rowbuf[0]])
                if last:
                    og = ostage[c]
                    P.dve(lambda e, xr=xr: e.tensor_tensor(out=og.ap, in0=xr.ap, in1=rowbuf[1].ap, op=ALU.add), [og], [xr, rowbuf[1]])
                    ob = Buf(out_d[t0 + c * 128:t0 + (c + 1) * 128, :])
                    P.dma("sp", ob.ap, og.ap, [ob], [og], st_sems[c])
                    out_bufs.append(ob)
                    if st_sems[c] not in pending_out:
                        pending_out.append(st_sems[c])
                else:
                    P.dve(lambda e, xr=xr: e.tensor_tensor(out=xr.ap, in0=xr.ap, in1=rowbuf[1].ap, op=ALU.add), [xr], [xr, rowbuf[1]])

        out_bufs = []
        st_sems = [P.new_sem("store") for _ in range(NT)]
        for ti in range(NTILES):
            for i, l in enumerate(layers):
                layer_tile(ti, i, l, first=(i == 0), last=(i == NL - 1))
        P.mark("end")
        P.wait_for("sp", out_bufs)
        P.finish()
    nc._marks = P.marks
    nc._tags = P.tags
    nc._names = P.names
    return nc


def _blk(w, c0):
    return np.ascontiguousarray(w[:, c0:c0 + 256].reshape(16, 128, 256).transpose(1, 0, 2)).reshape(128, 4096)


def prep_weights(inp):
    L = DEPTH
    w_in = inp["w_in"]
    w_in_r = np.empty((L * NWBLK, 128, 4096), np.float32)
    ws_r = np.empty((L, 128, 768), np.float32)
    wb_r = np.empty((L * 24, 128, 4096), np.float32)
    wo_r = np.empty((L * 8, 128, 4096), np.float32)
    wp_r = np.empty((L * 4, 128, 2048), np.float32)
    cols_l = np.empty((L, 128, 184), np.float32)
    srow_l = np.empty((L, 128, 112), np.float32)
    rows_l = np.empty((L * 4, 128, 2048), np.float32)
    small_cols = np.concatenate([np.arange(9216, 9248), np.arange(19488, 19504)])
    for l in range(L):
        for key in SEC:
            for j in range(SECN[key]):
                w_in_r[l * NWBLK + BLK0[key] + j] = _blk(w_in[l], SEC[key] + j * 256)
        ws = w_in[l][:, small_cols]
        ws_r[l] = ws.reshape(16, 128, 48).transpose(1, 0, 2).reshape(128, 768)
        for b in range(3):
            for j8 in range(8):
                wb_r[(l * 3 + b) * 8 + j8] = _blk(inp["w_branch"][l, b], j8 * 256)
        for n8 in range(8):
            wo_r[l * 8 + n8] = _blk(inp["w_out"][l], n8 * 256)
        for g in range(4):
            wp_r[l * 4 + g] = inp["w_pool"][l, g].reshape(4, 128, 512).transpose(1, 0, 2).reshape(128, 2048)
        cols_l[l, :, 0:48] = inp["b_gate"][l].reshape(3, 16, 128).transpose(2, 0, 1).reshape(128, 48)
        cols_l[l, :, 48:64] = inp["pool_scale"][l].reshape(16, 128).T
        cols_l[l, :, 64:160] = inp["conv_w"][l].reshape(4, 24, 128).transpose(2, 0, 1).reshape(128, 96)
        cols_l[l, :, 160:184] = inp["conv_b"][l].reshape(24, 128).T
        srow = np.concatenate([inp["dt_bias"][l], inp["a_log"][l], inp["d_skip"][l], inp["i_bias"][l], inp["f_bias"][l]])
        srow_l[l] = np.broadcast_to(srow[None, :], (128, 112))
        for k, name in enumerate(["ssm_norm_w", "mlstm_norm_w", "ln_g", "ln_b"]):
            rows_l[l * 4 + k] = np.broadcast_to(inp[name][l][None, :], (128, 2048))
    r = np.arange(128)
    consts = np.zeros((128, 704), np.float32)
    consts[:, 0:128] = np.eye(128)
    consts[:, 128:256] = (r[:, None] <= r[None, :])
    consts[:, 256:384] = (r[:, None] > r[None, :])
    consts[:, 384:512] = 1.0
    consts[:, 512:640] = np.where(r[None, :] < r[:, None], -30000.0, 0.0)
    for g in range(4):
        w = 2 ** (g + 1)
        consts[:, 640 + g * 16:640 + (g + 1) * 16] = 1.0 / np.minimum(np.arange(16) + 1, w)[None, :]
    return {"w_in_r": w_in_r, "ws_r": ws_r, "wb_r": wb_r, "wo_r": wo_r, "wp_r": wp_r, "cols_l": cols_l,
            "srow_l": srow_l, "rows_l": rows_l, "consts": consts}


TILE_T = 256


def kernel(**inputs):
    inp = {k: np.asarray(v, dtype=np.float32) for k, v in inputs.items()}
    x = inp["x"]
    shared = prep_weights(inp)
    nc = build_nc(list(range(DEPTH)), TILE_T)
    in_maps = []
    for b in range(8):
        m = dict(shared)
        m["x"] = np.ascontiguousarray(x[b])
        in_maps.append(m)
    res = run_bass_kernel_spmd(nc, in_maps, core_ids=list(range(8)))
    return np.stack([np.asarray(r["out"], dtype=np.float32) for r in res.results], axis=0)
```

```python
import contextlib
import numpy as np
import concourse.bass as bass
import concourse.mybir as mybir
from concourse.bass_utils import run_bass_kernel_spmd

F32 = mybir.dt.float32
BF16 = mybir.dt.bfloat16
AF = mybir.ActivationFunctionType
ALU = mybir.AluOpType

D = 2048
SEQ = 2048
NB = 16
IN_DIM = 25648
EPS = 1e-5
DEPTH = 2
ALPHA = (2 * DEPTH) ** 0.25
SEM_LIMIT = 30000
NSLOT = 4

SEC = {"pu": 0, "pz": 2048, "sx": 4096, "sz": 6144, "sB": 8192, "sC": 8704, "mq": 9248, "mk": 11296,
       "mv": 13344, "mo": 15392, "mz": 17440, "g": 19504}
SECN = {"pu": 8, "pz": 8, "sx": 8, "sz": 8, "sB": 2, "sC": 2, "mq": 8, "mk": 8, "mv": 8, "mo": 8, "mz": 8, "g": 24}
BLK0 = {}
_acc = 0
for _k in SEC:
    BLK0[_k] = _acc
    _acc += SECN[_k]
NWBLK = _acc


class Sem:
    __slots__ = ("h", "count", "owner")

    def __init__(self, h, owner):
        self.h = h
        self.count = 0
        self.owner = owner


class Buf:
    __slots__ = ("ap", "w", "r", "excl", "name")

    def __init__(self, ap, name="", excl=False):
        self.ap = ap
        self.w = None
        self.r = {}
        self.excl = excl
        self.name = name

    def __getitem__(self, k):
        return self.ap[k]


import sys as _sys


def _caller_line():
    f = _sys._getframe(2)
    while f is not None and f.f_code.co_name in ("emit", "pe", "act", "dve", "dma", "<lambda>", "proj_fm", "proj_tm", "to_fm", "rstd_from", "sp_load", "load_w"):
        f = f.f_back
    return f.f_lineno if f is not None else 0


class _Rec:
    def __getattr__(self, name):
        def f(*a, **k):
            return (name, a, k)
        return f


REC = _Rec()


class Prog:
    STREAMS = ("pe", "act", "dve", "pool", "sp")

    def __init__(self, nc, stack):
        self.nc = nc
        self.stack = stack
        self.ops = {s: [] for s in self.STREAMS}
        self.known = {s: {} for s in self.STREAMS}
        self.nsem = 0
        self.engsem = {s: self.new_sem(s) for s in ("pe", "act", "dve", "pool")}
        self.pending = {s: False for s in self.STREAMS}
        self.marks = []
        self.tags = {s: [] for s in self.STREAMS}
        self.names = {s: [] for s in self.STREAMS}

    def new_sem(self, owner):
        h = self.stack.enter_context(self.nc.semaphore(f"s{self.nsem}_{owner}"))
        self.nsem += 1
        return Sem(h, owner)

    def sbuf(self, shape, dtype, name):
        return self.stack.enter_context(self.nc.sbuf_tensor(name, list(shape), dtype))

    def psum(self, shape, dtype, name):
        return self.stack.enter_context(self.nc.psum_tensor(name, list(shape), dtype))

    def emit(self, stream, fn, outs=(), ins=(), inc=True, dma_sem=None):
        waits = {}
        known = self.known[stream]

        def need(sem, val):
            if stream == "pe" and sem.owner == "pe":
                return
            if known.get(sem, 0) >= val:
                return
            if waits.get(sem, 0) < val:
                waits[sem] = val

        for b in ins:
            if b.w is not None:
                need(*b.w)
            if b.excl:
                for sem, val in b.r.items():
                    need(sem, val)
        for b in outs:
            if b.w is not None:
                need(*b.w)
            for sem, val in b.r.items():
                need(sem, val)
        for sem, val in waits.items():
            known[sem] = val
        if fn is None:
            self.ops[stream].append((list(waits.items()), None, None, 0))
            return
        if dma_sem is not None:
            dma_sem.count += 16
            ev = (dma_sem, dma_sem.count)
            amt = 16
        else:
            es = self.engsem[stream]
            if es.count >= SEM_LIMIT and not self.pending[stream]:
                es = self.new_sem(stream)
                self.engsem[stream] = es
            if inc:
                es.count += 1
                ev = (es, es.count)
                self.pending[stream] = False
            else:
                ev = (es, es.count + 1)
                self.pending[stream] = True
            amt = 1 if inc else 0
        for b in ins:
            if b.excl:
                b.w = ev
                b.r = {}
            elif b.r.get(ev[0], 0) < ev[1]:
                b.r[ev[0]] = ev[1]
        for b in outs:
            b.w = ev
            b.r = {}
        self.ops[stream].append((list(waits.items()), fn(REC), ev[0], amt))
        self.tags[stream].append(_caller_line())

    def pe(self, fn, outs, ins, inc=True):
        self.emit("pe", fn, outs, ins, inc)

    def act(self, fn, outs, ins):
        self.emit("act", fn, outs, ins)

    def dve(self, fn, outs, ins):
        self.emit("dve", fn, outs, ins)

    def dma(self, stream, out_ap, in_ap, outs, ins, sem):
        self.emit(stream, lambda e: e.dma_start(out=out_ap, in_=in_ap), outs, ins, dma_sem=sem)

    def mark(self, name):
        self.marks.append((name, {t: sum(1 for o in self.ops[t] if o[1] is not None) for t in self.STREAMS}))

    def barrier(self, streams=("pe", "act", "dve"), extra=()):
        evs = [(self.engsem[t], self.engsem[t].count) for t in streams] + list(extra)
        for t in streams:
            assert not self.pending[t]
            waits = []
            for sem, val in evs:
                if val == 0 or (t == "pe" and sem.owner == "pe"):
                    continue
                if self.known[t].get(sem, 0) >= val:
                    continue
                self.known[t][sem] = val
                waits.append((sem, val))
            self.ops[t].append((waits, None, None, 0))

    def wait_for(self, stream, bufs):
        self.emit(stream, None, outs=bufs, ins=())

    def finish(self):
        nc = self.nc
        ops = self.ops
        with nc.Block() as block:
            def run(stream, e):
                for waits, fn, sem, amt in ops[stream]:
                    for wsem, val in waits:
                        e.wait_ge(wsem.h, val)
                    if fn is None:
                        continue
                    ins = getattr(e, fn[0])(*fn[1], **fn[2])
                    self.names[stream].append(ins.ins.name)
                    if amt:
                        ins.then_inc(sem.h, amt)

            @block.tensor
            def _(e):
                run("pe", e)

            @block.scalar
            def _(e):
                run("act", e)

            @block.vector
            def _(e):
                run("dve", e)

            @block.gpsimd
            def _(e):
                run("pool", e)

            @block.sync
            def _(e):
                run("sp", e)


def bc(ap, axis, shape):
    return ap.unsqueeze(axis).to_broadcast(list(shape))


def build_nc(layers, T, seq=SEQ):
    NT = T // 128
    NTILES = seq // T
    NL = len(layers)
    nc = bass.Bass("TRN2", target_bir_lowering=False)
    x_d = nc.dram_tensor("x", [seq, D], F32, kind="ExternalInput").ap()
    win_d = nc.dram_tensor("w_in_r", [DEPTH * NWBLK, 128, 4096], F32, kind="ExternalInput").ap()
    ws_d = nc.dram_tensor("ws_r", [DEPTH, 128, 768], F32, kind="ExternalInput").ap()
    wb_d = nc.dram_tensor("wb_r", [DEPTH * 24, 128, 4096], F32, kind="ExternalInput").ap()
    wo_d = nc.dram_tensor("wo_r", [DEPTH * 8, 128, 4096], F32, kind="ExternalInput").ap()
    wp_d = nc.dram_tensor("wp_r", [DEPTH * 4, 128, 2048], F32, kind="ExternalInput").ap()
    cols_d = nc.dram_tensor("cols_l", [DEPTH, 128, 184], F32, kind="ExternalInput").ap()
    srow_d = nc.dram_tensor("srow_l", [DEPTH, 128, 112], F32, kind="ExternalInput").ap()
    rows_d = nc.dram_tensor("rows_l", [DEPTH * 4, 128, 2048], F32, kind="ExternalInput").ap()
    cst_d = nc.dram_tensor("consts", [128, 704], F32, kind="ExternalInput").ap()
    out_d = nc.dram_tensor("out", [seq, D], F32, kind="ExternalOutput").ap()
    sc_d = {"in": nc.dram_tensor("wsc_in", [DEPTH * NWBLK, 128, 4096], BF16).ap(),
            "b": nc.dram_tensor("wsc_b", [DEPTH * 24, 128, 4096], BF16).ap(),
            "o": nc.dram_tensor("wsc_o", [DEPTH * 8, 128, 4096], BF16).ap(),
            "p": nc.dram_tensor("wsc_p", [DEPTH * 4, 128, 2048], BF16).ap()}
    src_d = {"in": win_d, "b": wb_d, "o": wo_d, "p": wp_d}

    with contextlib.ExitStack() as st:
        P = Prog(nc, st)

        def SB(shape, dt, name):
            return Buf(P.sbuf(shape, dt, name)[:], name)

        cst = SB([128, 704], F32, "cst")
        identb = SB([128, 128], BF16, "identb")
        cols = [SB([128, 184], F32, f"cols{i}") for i in range(NL)]
        srow = [SB([128, 112], F32, f"srow{i}") for i in range(NL)]
        Sst = [SB([128, 2048], F32, f"Sst{i}") for i in range(NL)]
        Cst = [[SB([128, 2, 257], F32, f"Cst{i}_{h}") for h in range(8)] for i in range(NL)]
        ctail = [SB([128, 24, 3], F32, f"ctail{i}") for i in range(NL)]
        ptail = [SB([128, 16, 16], F32, f"ptail{i}") for i in range(NL)]
        rowbuf = [SB([128, 2048], F32, f"rowbuf{i}") for i in range(2)]
        xres = [SB([128, 2048], F32, f"xres{c}") for c in range(NT)]
        xT = SB([128, NB, T], BF16, "xT")
        Y = [SB([128, NB, T], BF16, f"Y{b}") for b in range(3)]
        wslot = [SB([128, 4096], BF16, f"wslot{i}") for i in range(NSLOT)]
        wsm = SB([128, 768], BF16, "wsm")
        smalls = [SB([128, 128], F32, f"smalls{c}") for c in range(NT)]
        sexp = [SB([128, 128], F32, f"sexp{c}") for c in range(NT)]
        ss = SB([128, 8], F32, "ss")
        st6 = SB([128, 4, 6], F32, "st6")
        mv = SB([128, 2], F32, "mv")
        wgt = [SB([128, 8], F32, f"wgt{c}") for c in range(NT)]
        ARENA = 25600
        arena = P.sbuf([128, ARENA], BF16, "arena")[:]
        aoff = [0]

        pending_out = []

        def phase_begin(wait_out=False):
            if wait_out and pending_out:
                P.barrier(extra=[(sm_, sm_.count) for sm_ in pending_out])
                del pending_out[:]
            else:
                P.barrier()
            aoff[0] = 0

        def CV(shape, dt, name):
            n = 1
            for d_ in shape[1:]:
                n *= d_
            nb = n * (2 if dt == F32 else 1)
            nb = (nb + 1) // 2 * 2
            o = aoff[0]
            aoff[0] += nb
            assert aoff[0] <= ARENA, (name, aoff[0])
            ap = arena[:, o:o + nb]
            if dt == F32:
                ap = ap.bitcast(F32)
            ap = ap[:, 0:n]
            if len(shape) == 3:
                ap = ap.rearrange("p (a b) -> p a b", b=shape[2])
            return Buf(ap, name)

        mmb = [Buf(P.psum([128, 512], F32, f"mm{i}")[:], f"mm{i}", excl=True) for i in range(2)]
        smb = Buf(P.psum([128, 512], F32, "smb")[:], "smb", excl=True)
        dps = Buf(P.psum([128, 1024], F32, "dps")[:], "dps", excl=True)
        qkb = Buf(P.psum([128, 512], F32, "qkb")[:], "qkb", excl=True)
        totb = Buf(P.psum([128, 512], F32, "totb")[:], "totb", excl=True)
        trb = Buf(P.psum([128, 1024], BF16, "trb")[:], "trb", excl=True)
        mm_i = [0]

        def mm_bank():
            mm_i[0] ^= 1
            return mmb[mm_i[0]]

        ident = cst.ap[:, 0:128]
        tri_le = cst.ap[:, 128:256]
        mask_gt = cst.ap[:, 256:384]
        ones = cst.ap[:, 384:512]
        negmask = cst.ap[:, 512:640]
        icnt = cst.ap[:, 640:704]

        sp_sems = {}

        def sp_sem_for(buf):
            if id(buf) not in sp_sems:
                sp_sems[id(buf)] = P.new_sem("dma")
            return sp_sems[id(buf)]

        def sp_load(dst_buf, dst_ap, src_ap):
            P.dma("sp", dst_ap, src_ap, [dst_buf], [], sp_sem_for(dst_buf))

        wsem = [P.new_sem("wdma") for _ in range(NSLOT)]
        wsem_sw = [P.new_sem("wdma_sw") for _ in range(NSLOT)]
        wsm_sem = P.new_sem("wsm")
        w_i = [0]

        sc_bufs = {}
        wbsem = [P.new_sem("wback") for _ in range(NSLOT)]

        def load_w(kind, idx, n=4096):
            i = w_i[0] % NSLOT
            w_i[0] += 1
            b = wslot[i]
            key = (kind, idx)
            if key not in sc_bufs:
                P.dma("pool", b.ap[:, 0:n], src_d[kind][idx], [b], [], wsem_sw[i])
                sb = Buf(sc_d[kind][idx], f"sc_{kind}{idx}")
                sc_bufs[key] = sb
                P.dma("sp", sb.ap, b.ap[:, 0:n], [sb], [b], wbsem[i])
            else:
                P.dma("sp", b.ap[:, 0:n], sc_bufs[key].ap, [b], [sc_bufs[key]], wsem[i])
            return b

        def wv(b, F):
            return b.ap[:, 0:16 * F].rearrange("p (k f) -> p k f", f=F)

        sp_load(cst, cst.ap, cst_d)
        for i, l in enumerate(layers):
            sp_load(cols[i], cols[i].ap, cols_d[l])
            sp_load(srow[i], srow[i].ap, srow_d[l])
        P.dve(lambda e: e.tensor_copy(out=identb.ap, in_=ident), [identb], [cst])
        for i in range(NL):
            P.act(lambda e, i=i: e.activation(out=srow[i].ap[:, 32:64], in_=srow[i].ap[:, 32:64], func=AF.Exp), [srow[i]], [srow[i]])
            P.dve(lambda e, i=i: e.tensor_scalar(out=srow[i].ap[:, 32:64], in0=srow[i].ap[:, 32:64], scalar1=-1.0, scalar2=None, op0=ALU.mult), [srow[i]], [srow[i]])
            P.dve(lambda e, i=i: e.memset(Sst[i].ap, 0.0), [Sst[i]], [])
            for h in range(8):
                P.dve(lambda e, i=i, h=h: e.memset(Cst[i][h].ap, 0.0), [Cst[i][h]], [])
            P.dve(lambda e, i=i: e.memset(ctail[i].ap, 0.0), [ctail[i]], [])
            P.dve(lambda e, i=i: e.memset(ptail[i].ap, 0.0), [ptail[i]], [])

        def to_fm(src_buf, src_ap_fn, nblk, dst_buf, dst_ap_fn, extra_ins=()):
            j0 = 0
            while j0 < nblk:
                n = min(8, nblk - j0)
                for j in range(n):
                    P.pe(lambda e, j=j, j0=j0: e.transpose(trb.ap[:, j * 128:(j + 1) * 128], src_ap_fn(j0 + j), identb.ap),
                         [trb], [src_buf, identb], inc=(j == n - 1))
                P.act(lambda e, j0=j0, n=n: e.activation(out=dst_ap_fn(j0, n), in_=trb.ap[:, 0:n * 128].rearrange("p (a b) -> p a b", b=128), func=AF.Copy),
                      [dst_buf], [trb])
                j0 += n

        def proj_fm(wbuf, F, f0, rhs_buf, evac):
            w3 = wv(wbuf, F)
            ps = mm_bank()
            for kc in range(NB):
                P.pe(lambda e, kc=kc: e.matmul(ps.ap[:, 0:T], lhsT=w3[:, kc, f0:f0 + 128], rhs=rhs_buf.ap[:, kc, :], start=(kc == 0), stop=(kc == NB - 1)),
                     [ps], [wbuf, rhs_buf], inc=(kc == NB - 1))
            evac(ps)

        def proj_tm(wbufs, F, nf, c, lhs_buf, evac):
            ps = mm_bank()
            for wi, wbuf in enumerate(wbufs):
                w3 = wv(wbuf, F)
                for kc in range(NB):
                    P.pe(lambda e, kc=kc, w3=w3, wi=wi: e.matmul(ps.ap[:, wi * nf:(wi + 1) * nf], lhsT=lhs_buf.ap[:, kc, c * 128:(c + 1) * 128], rhs=w3[:, kc, 0:nf], start=(kc == 0), stop=(kc == NB - 1)),
                         [ps], [wbuf, lhs_buf], inc=(kc == NB - 1))
            evac(ps)

        def rstd_from(var_ap, buf, scale=1.0):
            P.act(lambda e: e.activation(out=var_ap, in_=var_ap, func=AF.Ln, bias=EPS, scale=scale), [buf], [buf])
            P.act(lambda e: e.activation(out=var_ap, in_=var_ap, func=AF.Exp, scale=-0.5), [buf], [buf])

        def run_threads(*gens):
            gens = [g_ for g_ in gens if g_ is not None]
            while gens:
                for g_ in list(gens):
                    try:
                        next(g_)
                    except StopIteration:
                        gens.remove(g_)
                    assert not P.pending["pe"]

        def layer_tile(ti, i, l, first, last):
            t0 = ti * T
            WIN = lambda key, j: ("in", l * NWBLK + BLK0[key] + j)
            cl = cols[i]
            sr = srow[i]
            P.mark(f"t{ti}l{l}:0")
            phase_begin()
            xb = CV([128, 2048], BF16, "xb")
            for c in range(NT):
                if first:
                    sp_load(xres[c], xres[c].ap, x_d[t0 + c * 128:t0 + (c + 1) * 128, :])
                P.act(lambda e, c=c: e.activation(out=xb.ap, in_=xres[c].ap, func=AF.Copy), [xb], [xres[c]])
                to_fm(xb, lambda j: xb.ap[:, j * 128:(j + 1) * 128], NB, xT,
                      lambda j0, n, c=c: xT.ap[:, j0:j0 + n, c * 128:(c + 1) * 128])

            P.dma("pool", wsm.ap, ws_d[l], [wsm], [], wsm_sem)
            wsm3 = wsm.ap.rearrange("p (k f) -> p k f", f=48)
            for c in range(NT):
                sm = smalls[c]
                for kc in range(NB):
                    P.pe(lambda e, kc=kc, c=c: e.matmul(smb.ap[:, 0:48], lhsT=xT.ap[:, kc, c * 128:(c + 1) * 128], rhs=wsm3[:, kc, :], start=(kc == 0), stop=(kc == NB - 1)),
                         [smb], [wsm, xT], inc=(kc == NB - 1))
                P.dve(lambda e, sm=sm: e.tensor_tensor(out=sm.ap[:, 0:32], in0=smb.ap[:, 0:32], in1=sr.ap[:, 0:32], op=ALU.add), [sm], [smb, sr])
                P.dve(lambda e, sm=sm: e.tensor_tensor(out=sm.ap[:, 96:104], in0=smb.ap[:, 32:40], in1=sr.ap[:, 96:104], op=ALU.add), [sm], [smb, sr])
                P.dve(lambda e, sm=sm: e.tensor_tensor(out=sm.ap[:, 104:112], in0=smb.ap[:, 40:48], in1=sr.ap[:, 104:112], op=ALU.add), [sm], [smb, sr])
                P.act(lambda e, sm=sm: e.activation(out=sm.ap[:, 0:32], in_=sm.ap[:, 0:32], func=AF.Exp), [sm], [sm])
                P.act(lambda e, sm=sm: e.activation(out=sm.ap[:, 0:32], in_=sm.ap[:, 0:32], func=AF.Ln, bias=1.0, scale=1.0), [sm], [sm])
                P.act(lambda e, sm=sm: e.activation(out=sm.ap[:, 32:64], in_=sm.ap[:, 0:32], func=AF.Ln), [sm], [sm])
                P.dve(lambda e, sm=sm: e.tensor_tensor(out=sm.ap[:, 64:96], in0=sm.ap[:, 0:32], in1=sr.ap[:, 32:64], op=ALU.mult), [sm], [sm, sr])
                P.act(lambda e, sm=sm: e.activation(out=sm.ap[:, 104:112], in_=sm.ap[:, 104:112], func=AF.Exp, scale=-1.0), [sm], [sm])
                P.act(lambda e, sm=sm: e.activation(out=sm.ap[:, 104:112], in_=sm.ap[:, 104:112], func=AF.Ln, bias=1.0, scale=1.0), [sm], [sm])
                P.dve(lambda e, sm=sm: e.tensor_scalar(out=sm.ap[:, 104:112], in0=sm.ap[:, 104:112], scalar1=-1.0, scalar2=None, op0=ALU.mult), [sm], [sm])
                a_ap = sm.ap[:, 64:96]
                P.pe(lambda e, a_ap=a_ap: e.matmul(smb.ap[:, 0:32], lhsT=tri_le, rhs=a_ap, start=True, stop=True), [smb], [cst, sm], inc=False)
                P.pe(lambda e, a_ap=a_ap: e.matmul(smb.ap[:, 32:64], lhsT=mask_gt, rhs=a_ap, start=True, stop=True), [smb], [cst, sm], inc=False)
                P.pe(lambda e, a_ap=a_ap: e.matmul(smb.ap[:, 64:96], lhsT=ones, rhs=a_ap, start=True, stop=True), [smb], [cst, sm], inc=True)
                P.act(lambda e, c=c: e.activation(out=sexp[c].ap[:, 0:96], in_=smb.ap[:, 0:96], func=AF.Exp), [sexp[c]], [smb])
                P.dve(lambda e, c=c, sm=sm: e.tensor_tensor(out=sexp[c].ap[:, 96:128], in0=sexp[c].ap[:, 32:64], in1=sm.ap[:, 0:32], op=ALU.mult), [sexp[c]], [sexp[c], sm])

            P.mark(f"t{ti}l{l}:A")
            phase_begin(wait_out=True)
            pA = CV([128, 16 + T], F32, "pA")
            pB = CV([128, 16 + T], F32, "pB")
            pC = CV([128, 16 + T], F32, "pC")
            p16 = CV([128, 16], F32, "p16")
            pooled = CV([128, 4, T], BF16, "pooled")
            spz = CV([128, 4, T], BF16, "spz")
            for g in range(4):
                nsteps = g + 1
                wwin = 2 ** (g + 1)
                for jb in range(2):
                    wu = load_w(*WIN("pu", g * 2 + jb))
                    wz = load_w(*WIN("pz", g * 2 + jb))
                    for m in range(2):
                        q = jb * 2 + m
                        blk = g * 4 + q

                        def evac_pu(ps, q=q, blk=blk):
                            if ti == 0:
                                P.dve(lambda e: e.memset(pA.ap[:, 0:16], 0.0), [pA], [])
                            else:
                                P.dve(lambda e: e.tensor_copy(out=pA.ap[:, 0:16], in_=ptail[i].ap[:, blk, :]), [pA], [ptail[i]])
                            P.act(lambda e: e.activation(out=pA.ap[:, 16:16 + T], in_=ps.ap[:, 0:T], func=AF.Copy), [pA], [ps])
                            P.dve(lambda e: e.tensor_copy(out=ptail[i].ap[:, blk, :], in_=pA.ap[:, T:T + 16]), [ptail[i]], [pA])
                            cur = pA
                            lo = 0
                            for k in range(nsteps):
                                stp = 2 ** k
                                lo += stp
                                dst = pB if k % 2 == 0 else pC
                                P.dve(lambda e, cur=cur, dst=dst, lo=lo, stp=stp: e.tensor_tensor(out=dst.ap[:, lo:16 + T], in0=cur.ap[:, lo:16 + T], in1=cur.ap[:, lo - stp:16 + T - stp], op=ALU.add),
                                      [dst], [cur])
                                cur = dst
                            P.dve(lambda e, cur=cur: e.scalar_tensor_tensor(out=pooled.ap[:, q, :], in0=cur.ap[:, 16:16 + T], scalar=1.0 / wwin, in1=pA.ap[:, 16:16 + T], op0=ALU.mult, op1=ALU.subtract),
                                  [pooled], [cur, pA])
                            if ti == 0:
                                P.dve(lambda e, cur=cur: e.tensor_tensor(out=p16.ap, in0=cur.ap[:, 16:32], in1=icnt[:, g * 16:(g + 1) * 16], op=ALU.mult), [p16], [cur, cst])
                                P.dve(lambda e: e.tensor_tensor(out=pooled.ap[:, q, 0:16], in0=p16.ap, in1=pA.ap[:, 16:32], op=ALU.subtract), [pooled], [p16, pA])

                        proj_fm(wu, 256, m * 128, xT, evac_pu)
                        proj_fm(wz, 256, m * 128, xT,
                                lambda ps, q=q: P.act(lambda e: e.activation(out=spz.ap[:, q, :], in_=ps.ap[:, 0:T], func=AF.Silu), [spz], [ps]))
                wp = load_w("p", l * 4 + g, 2048)
                wp3 = wp.ap[:, 0:2048].rearrange("p (k f) -> p k f", f=512)
                for m in range(4):
                    ps = mm_bank()
                    for kc in range(4):
                        P.pe(lambda e, kc=kc, m=m, ps=ps: e.matmul(ps.ap[:, 0:T], lhsT=wp3[:, kc, m * 128:(m + 1) * 128], rhs=pooled.ap[:, kc, :], start=(kc == 0), stop=(kc == 3)),
                             [ps], [wp, pooled], inc=(kc == 3))
                    P.dve(lambda e, m=m, ps=ps: e.scalar_tensor_tensor(out=Y[0].ap[:, g * 4 + m, :], in0=ps.ap[:, 0:T], scalar=cl.ap[:, 48 + g * 4 + m:48 + g * 4 + m + 1], in1=spz.ap[:, m, :], op0=ALU.mult, op1=ALU.mult),
                          [Y[0]], [ps, cl, spz])

            P.mark(f"t{ti}l{l}:B")
            phase_begin()
            craw2 = [CV([128, 4 + T], F32, f"craw{k_}") for k_ in range(2)]
            cacc2 = [CV([128, T], F32, f"cacc{k_}") for k_ in range(2)]
            cv_i = [0]
            cv_pend = []

            def conv_flush():
                while cv_pend:
                    cb_, oap_, ob_ = cv_pend.pop(0)
                    P.act(lambda e: e.activation(out=oap_, in_=cb_.ap, func=AF.Silu), [ob_], [cb_])
            xfm = CV([128, 4, T], BF16, "xfm")
            BT = [CV([128, T], BF16, f"BT{k_}") for k_ in range(4)]
            CT = [CV([128, T], BF16, f"CT{k_}") for k_ in range(4)]
            Xtok2 = [CV([128, NT, 512], BF16, f"Xtok{k_}") for k_ in range(2)]
            Btok2 = [CV([128, NT, 128], BF16, f"Btok{k_}") for k_ in range(2)]
            szg2 = [CV([128, NT, 512], F32, f"szg{k_}") for k_ in range(2)]
            LT = CV([128, 8, 128], F32, "LT")
            ADDt = CV([128, 8, 128], F32, "ADDt")
            MT4 = [CV([128, 8, 128], BF16, f"MT{k_}") for k_ in range(2 * NT)]
            Sbf = CV([128, 512], BF16, "Sbf")
            y1 = CV([128, 512], F32, "y1")
            y2 = CV([128, 512], F32, "y2")
            y3 = CV([128, 512], F32, "y3")
            ytok = CV([128, 512], BF16, "ytok")
            Xd = CV([128, 512], BF16, "Xd")
            sp_load(rowbuf[0], rowbuf[0].ap, rows_d[l * 4 + 0])

            def conv_block(ps, cb, out_ap, out_buf):
                cv_i[0] ^= 1
                craw, cacc = craw2[cv_i[0]], cacc2[cv_i[0]]
                P.dve(lambda e: e.tensor_copy(out=craw.ap[:, 0:3], in_=ctail[i].ap[:, cb, :]), [craw], [ctail[i]])
                P.act(lambda e: e.activation(out=craw.ap[:, 3:3 + T], in_=ps.ap[:, 0:T], func=AF.Copy), [craw], [ps])
                P.dve(lambda e: e.tensor_copy(out=ctail[i].ap[:, cb, :], in_=craw.ap[:, T:T + 3]), [ctail[i]], [craw])
                wcol = lambda j: cl.ap[:, 64 + j * 24 + cb:64 + j * 24 + cb + 1]
                bcol = cl.ap[:, 160 + cb:160 + cb + 1]
                P.dve(lambda e: e.tensor_scalar(out=cacc.ap, in0=craw.ap[:, 3:3 + T], scalar1=wcol(3), scalar2=bcol, op0=ALU.mult, op1=ALU.add), [cacc], [craw, cl])
                for j in range(3):
                    P.dve(lambda e, j=j: e.scalar_tensor_tensor(out=cacc.ap, in0=craw.ap[:, j:j + T], scalar=wcol(j), in1=cacc.ap, op0=ALU.mult, op1=ALU.add), [cacc], [craw, cl, cacc])
                conv_flush()
                cv_pend.append((cacc, out_ap, out_buf))

            v3 = lambda ap: ap.rearrange("p (a b) -> p a b", b=64)

            def gen_proj_BC(g):
                if g < 4 and g % 2 == 0:
                    wB = load_w(*WIN("sB", g // 2))
                    wC = load_w(*WIN("sC", g // 2))
                    for m in range(2):
                        proj_fm(wB, 256, m * 128, xT, lambda ps, m=m: conv_block(ps, 16 + g + m, BT[g + m].ap, BT[g + m]))
                        yield
                        proj_fm(wC, 256, m * 128, xT, lambda ps, m=m: conv_block(ps, 20 + g + m, CT[g + m].ap, CT[g + m]))
                        if m == 1:
                            conv_flush()
                        yield

            def gen_proj_X(g):
                pb = g % 2
                Xtok, Btok, szg = Xtok2[pb], Btok2[pb], szg2[pb]
                for jb in range(2):
                    wx = load_w(*WIN("sx", g * 2 + jb))
                    for m in range(2):
                        q = jb * 2 + m
                        proj_fm(wx, 256, m * 128, xT, lambda ps, q=q: conv_block(ps, g * 4 + q, xfm.ap[:, q, :], xfm))
                        if q == 3:
                            conv_flush()
                        yield
                for c in range(NT):
                    to_fm(xfm, lambda j, c=c: xfm.ap[:, j, c * 128:(c + 1) * 128], 4, Xtok,
                          lambda j0, n, c=c: Xtok.ap[:, c, :].rearrange("p (a b) -> p a b", b=128))
                    to_fm(BT[g], lambda j, c=c: BT[g].ap[:, c * 128:(c + 1) * 128], 1, Btok,
                          lambda j0, n, c=c: Btok.ap[:, c, :].rearrange("p (a b) -> p a b", b=128))
                    yield
                wz0 = load_w(*WIN("sz", g * 2))
                wz1 = load_w(*WIN("sz", g * 2 + 1))
                for c in range(NT):
                    proj_tm([wz0, wz1], 256, 256, c, xT,
                            lambda ps, c=c: P.act(lambda e: e.activation(out=szg.ap[:, c, :], in_=ps.ap[:, 0:512], func=AF.Silu), [szg], [ps]))
                    yield

            def gen_front_B(g):
                BTg, CTg = BT[g], CT[g]
                for c in range(NT):
                    MT = MT4[(g % 2) * NT + c]
                    sm = smalls[c]
                    a_g = sm.ap[:, 64 + g * 8:64 + (g + 1) * 8]
                    lndt_g = sm.ap[:, 32 + g * 8:32 + (g + 1) * 8]
                    cs = slice(c * 128, (c + 1) * 128)
                    P.dve(lambda e: e.tensor_tensor(out=LT.ap, in0=bc(tri_le, 1, [128, 8, 128]), in1=bc(a_g, 2, [128, 8, 128]), op=ALU.mult), [LT], [cst, sm])
                    P.dve(lambda e: e.tensor_tensor(out=ADDt.ap, in0=bc(negmask, 1, [128, 8, 128]), in1=bc(lndt_g, 2, [128, 8, 128]), op=ALU.add), [ADDt], [cst, sm])
                    LT2 = LT.ap.rearrange("p a b -> p (a b)")
                    AD2 = ADDt.ap.rearrange("p a b -> p (a b)")
                    for hh in range(2):
                        P.pe(lambda e: e.matmul(dps.ap[:, hh * 512:(hh + 1) * 512], lhsT=mask_gt, rhs=LT2[:, hh * 512:(hh + 1) * 512], start=True, stop=False), [dps], [cst, LT], inc=False)
                        P.pe(lambda e: e.matmul(dps.ap[:, hh * 512:(hh + 1) * 512], lhsT=ident, rhs=AD2[:, hh * 512:(hh + 1) * 512], start=False, stop=True), [dps], [cst, ADDt], inc=(hh == 1))
                    P.pe(lambda e: e.matmul(qkb.ap[:, 0:128], lhsT=BTg.ap[:, cs], rhs=CTg.ap[:, cs], start=True, stop=True), [qkb], [BTg, CTg])
                    yield
                    P.act(lambda e: e.activation(out=LT2, in_=dps.ap, func=AF.Exp), [LT], [dps])
                    P.dve(lambda e: e.tensor_tensor(out=MT.ap, in0=LT.ap, in1=bc(qkb.ap[:, 0:128], 1, [128, 8, 128]), op=ALU.mult), [MT], [LT, qkb])
                    yield

            def gen_back_B(g):
                pb = g % 2
                Xtok, Btok, szg = Xtok2[pb], Btok2[pb], szg2[pb]
                CTg = CT[g]
                Sg = Sst[i].ap[:, g * 512:(g + 1) * 512]
                dsk = sr.ap[:, 64 + g * 8:64 + (g + 1) * 8]

                def u1(c):
                    MT = MT4[(g % 2) * NT + c]
                    dd_g = sexp[c].ap[:, 96 + g * 8:96 + (g + 1) * 8]
                    cs = slice(c * 128, (c + 1) * 128)
                    P.dve(lambda e: e.tensor_copy(out=Sbf.ap, in_=Sg), [Sbf], [Sst[i]])
                    P.pe(lambda e: e.matmul(smb.ap[:, 0:512], lhsT=CTg.ap[:, cs], rhs=Sbf.ap, start=True, stop=True), [smb], [CTg, Sbf])
                    for h in range(8):
                        P.pe(lambda e: e.matmul(totb.ap[:, h * 64:(h + 1) * 64], lhsT=MT.ap[:, h, :], rhs=Xtok.ap[:, c, h * 64:(h + 1) * 64], start=True, stop=True), [totb], [MT, Xtok], inc=(h == 7))
                    P.dve(lambda e: e.tensor_tensor(out=v3(Xd.ap), in0=v3(Xtok.ap[:, c, :]), in1=bc(dd_g, 2, [128, 8, 64]), op=ALU.mult), [Xd], [Xtok, sexp[c]])

                def u2(c):
                    se = sexp[c]
                    eA_g = se.ap[:, g * 8:(g + 1) * 8]
                    cd_g = se.ap[:, 64 + g * 8:64 + (g + 1) * 8]
                    P.dve(lambda e: e.tensor_tensor(out=v3(y1.ap), in0=v3(smb.ap[:, 0:512]), in1=bc(eA_g, 2, [128, 8, 64]), op=ALU.mult), [y1], [smb, se])
                    P.pe(lambda e: e.matmul(smb.ap[:, 0:512], lhsT=Btok.ap[:, c, :], rhs=Xd.ap, start=True, stop=True), [smb], [Btok, Xd])
                    P.dve(lambda e: e.tensor_tensor(out=y2.ap, in0=totb.ap[:, 0:512], in1=y1.ap, op=ALU.add), [y2], [totb, y1])
                    P.dve(lambda e: e.tensor_tensor(out=v3(y1.ap), in0=v3(Xtok.ap[:, c, :]), in1=bc(dsk, 2, [128, 8, 64]), op=ALU.mult), [y1], [Xtok, sr])
                    P.dve(lambda e: e.tensor_tensor(out=y2.ap, in0=y2.ap, in1=y1.ap, op=ALU.add), [y2], [y2, y1])
                    P.dve(lambda e: e.tensor_tensor(out=y2.ap, in0=y2.ap, in1=szg.ap[:, c, :], op=ALU.mult), [y2], [y2, szg])
                    P.act(lambda e: e.activation(out=y1.ap, in_=y2.ap, func=AF.Square, accum_out=ss.ap[:, 0:1]), [y1, ss], [y2])
                    P.dve(lambda e: e.tensor_tensor(out=v3(y3.ap), in0=v3(Sg), in1=bc(cd_g, 2, [128, 8, 64]), op=ALU.mult), [y3], [Sst[i], se])
                    P.dve(lambda e: e.tensor_tensor(out=Sg, in0=smb.ap[:, 0:512], in1=y3.ap, op=ALU.add), [Sst[i]], [smb, y3])

                def u3(c):
                    rstd_from(ss.ap[:, 0:1], ss, scale=1.0 / 512)
                    P.dve(lambda e: e.scalar_tensor_tensor(out=ytok.ap, in0=y2.ap, scalar=ss.ap[:, 0:1], in1=rowbuf[0].ap[:, g * 512:(g + 1) * 512], op0=ALU.mult, op1=ALU.mult), [ytok], [y2, ss, rowbuf[0]])
                    to_fm(ytok, lambda j: ytok.ap[:, j * 128:(j + 1) * 128], 4, Y[1],
                          lambda j0, n: Y[1].ap[:, g * 4:g * 4 + 4, c * 128:(c + 1) * 128])

                u1(0)
                yield
                for c in range(NT):
                    u2(c)
                    yield
                    if c + 1 < NT:
                        u1(c + 1)
                        yield
                    u3(c)
                    yield

            def g_seq(*gens):
                for g_ in gens:
                    if g_ is not None:
                        yield from g_

            def g_par(*gens):
                gens = [g_ for g_ in gens if g_ is not None]
                while gens:
                    for g_ in list(gens):
                        try:
                            next(g_)
                            yield
                        except StopIteration:
                            gens.remove(g_)

            run_threads(gen_proj_BC(0))
            run_threads(gen_front_B(0), gen_proj_X(0))
            for g in range(4):
                if g < 3:
                    run_threads(gen_back_B(g), g_seq(gen_proj_BC(g + 1), g_par(gen_front_B(g + 1), gen_proj_X(g + 1))))
                else:
                    run_threads(gen_back_B(g))

            P.mark(f"t{ti}l{l}:C")
            phase_begin()
            EB = [CV([128, 8, 128], F32, f"EB{c}") for c in range(NT)]
            EE = [CV([128, 8, 128], F32, f"EE{c}") for c in range(NT)]
            PT = CV([128, 128], BF16, "PT")
            qp = CV([128, 2, 128], BF16, "qp")
            Cbf = CV([128, 2, 258], BF16, "Cbf")
            hc = CV([128, 256], F32, "hc")
            hy = CV([128, 256], F32, "hy")
            hyb = CV([128, 256], BF16, "hyb")

            def cset(k_):
                return dict(qT=CV([128, 2, T], BF16, f"qT{k_}"), kT=CV([128, 2, T], BF16, f"kT{k_}"),
                            vaug=CV([128, NT, 258], BF16, f"vaug{k_}"), sigo=CV([128, NT, 256], F32, f"sigo{k_}"),
                            silz=CV([128, NT, 256], F32, f"silz{k_}"), kw=CV([128, NT, 256], BF16, f"kw{k_}"))

            cs0 = cset(0)
            ov = aoff[0]
            LT = CV([128, 8, 128], F32, "LT")
            ADDt = CV([128, 8, 128], F32, "ADDt")
            sp_load(rowbuf[1], rowbuf[1].ap, rows_d[l * 4 + 1])
            for c in range(NT):
                sm = smalls[c]
                ig = sm.ap[:, 96:104]
                lf = sm.ap[:, 104:112]
                P.pe(lambda e: e.matmul(smb.ap[:, 0:8], lhsT=mask_gt, rhs=lf, start=True, stop=False), [smb], [cst, sm], inc=False)
                P.pe(lambda e: e.matmul(smb.ap[:, 0:8], lhsT=ident, rhs=ig, start=False, stop=True), [smb], [cst, sm])
                P.act(lambda e: e.activation(out=wgt[c].ap, in_=smb.ap[:, 0:8], func=AF.Exp), [wgt[c]], [smb])
                P.dve(lambda e: e.tensor_tensor(out=LT.ap, in0=bc(tri_le, 1, [128, 8, 128]), in1=bc(lf, 2, [128, 8, 128]), op=ALU.mult), [LT], [cst, sm])
                P.dve(lambda e: e.tensor_tensor(out=ADDt.ap, in0=bc(negmask, 1, [128, 8, 128]), in1=bc(ig, 2, [128, 8, 128]), op=ALU.add), [ADDt], [cst, sm])
                LT2 = LT.ap.rearrange("p a b -> p (a b)")
                AD2 = ADDt.ap.rearrange("p a b -> p (a b)")
                for hh in range(2):
                    P.pe(lambda e: e.matmul(dps.ap[:, hh * 512:(hh + 1) * 512], lhsT=ones, rhs=LT2[:, hh * 512:(hh + 1) * 512], start=True, stop=True), [dps], [cst, LT], inc=(hh == 1))
                P.act(lambda e: e.activation(out=EB[c].ap.rearrange("p a b -> p (a b)"), in_=dps.ap, func=AF.Exp), [EB[c]], [dps])
                for hh in range(2):
                    P.pe(lambda e: e.matmul(dps.ap[:, hh * 512:(hh + 1) * 512], lhsT=mask_gt, rhs=LT2[:, hh * 512:(hh + 1) * 512], start=True, stop=False), [dps], [cst, LT], inc=False)
                    P.pe(lambda e: e.matmul(dps.ap[:, hh * 512:(hh + 1) * 512], lhsT=ident, rhs=AD2[:, hh * 512:(hh + 1) * 512], start=False, stop=True), [dps], [cst, ADDt], inc=(hh == 1))
                P.act(lambda e: e.activation(out=EE[c].ap.rearrange("p a b -> p (a b)"), in_=dps.ap, func=AF.Exp), [EE[c]], [dps])
            P.barrier()
            aoff[0] = ov
            cs1 = cset(1)
            csets = [cs0, cs1]
            for k_ in range(2):
                P.dve(lambda e: e.memset(csets[k_]["vaug"].ap[:, :, 256:257], 1.0), [csets[k_]["vaug"]], [])

            def gen_proj_C(h):
                S_ = csets[h % 2]
                qT, kT, vaug, sigo, silz, kw = S_["qT"], S_["kT"], S_["vaug"], S_["sigo"], S_["silz"], S_["kw"]
                wq = load_w(*WIN("mq", h))
                wk = load_w(*WIN("mk", h))
                for m in range(2):
                    proj_fm(wq, 256, m * 128, xT, lambda ps, m=m: P.act(lambda e: e.activation(out=qT.ap[:, m, :], in_=ps.ap[:, 0:T], func=AF.Copy), [qT], [ps]))
                    yield
                    proj_fm(wk, 256, m * 128, xT, lambda ps, m=m: P.act(lambda e: e.activation(out=kT.ap[:, m, :], in_=ps.ap[:, 0:T], func=AF.Copy, scale=0.0625), [kT], [ps]))
                    yield
                wvv = load_w(*WIN("mv", h))
                wo_ = load_w(*WIN("mo", h))
                for c in range(NT):
                    proj_tm([wvv], 256, 256, c, xT, lambda ps, c=c: P.act(lambda e: e.activation(out=vaug.ap[:, c, 0:256], in_=ps.ap[:, 0:256], func=AF.Copy), [vaug], [ps]))
                    yield
                    proj_tm([wo_], 256, 256, c, xT, lambda ps, c=c: P.act(lambda e: e.activation(out=sigo.ap[:, c, :], in_=ps.ap[:, 0:256], func=AF.Sigmoid), [sigo], [ps]))
                    yield
                wzz = load_w(*WIN("mz", h))
                for c in range(NT):
                    proj_tm([wzz], 256, 256, c, xT, lambda ps, c=c: P.act(lambda e: e.activation(out=silz.ap[:, c, :], in_=ps.ap[:, 0:256], func=AF.Silu), [silz], [ps]))
                    P.dve(lambda e: e.tensor_tensor(out=silz.ap[:, c, :], in0=silz.ap[:, c, :], in1=rowbuf[1].ap[:, h * 256:(h + 1) * 256], op=ALU.mult), [silz], [silz, rowbuf[1]])
                    yield
                for c in range(NT):
                    cs = slice(c * 128, (c + 1) * 128)
                    for m in range(2):
                        P.pe(lambda e: e.transpose(trb.ap[:, m * 128:(m + 1) * 128], kT.ap[:, m, cs], identb.ap), [trb], [kT, identb], inc=(m == 1))
                    P.dve(lambda e: e.tensor_scalar(out=kw.ap[:, c, :], in0=trb.ap[:, 0:256], scalar1=wgt[c].ap[:, h:h + 1], scalar2=None, op0=ALU.mult), [kw], [trb, wgt[c]])
                    yield

            def gen_core_C(h):
                S_ = csets[h % 2]
                qT, kT, vaug, sigo, silz, kw = S_["qT"], S_["kT"], S_["vaug"], S_["sigo"], S_["silz"], S_["kw"]
                def u1a(c):
                    cs = slice(c * 128, (c + 1) * 128)
                    for m in range(2):
                        P.pe(lambda e: e.matmul(qkb.ap[:, 0:128], lhsT=kT.ap[:, m, cs], rhs=qT.ap[:, m, cs], start=(m == 0), stop=(m == 1)), [qkb], [kT, qT], inc=(m == 1))
                    P.dve(lambda e: e.tensor_tensor(out=qp.ap, in0=qT.ap[:, :, cs], in1=bc(EB[c].ap[:, h, :], 1, [128, 2, 128]), op=ALU.mult), [qp], [qT, EB[c]])

                u1a(0)
                for c in range(NT):
                    cs = slice(c * 128, (c + 1) * 128)
                    P.dve(lambda e: e.tensor_copy(out=Cbf.ap[:, :, 0:257], in_=Cst[i][h].ap), [Cbf], [Cst[i][h]])
                    yield
                    P.dve(lambda e: e.tensor_tensor(out=PT.ap, in0=qkb.ap[:, 0:128], in1=EE[c].ap[:, h, :], op=ALU.mult), [PT], [qkb, EE[c]])
                    P.pe(lambda e: e.matmul(totb.ap[:, 0:257], lhsT=PT.ap, rhs=vaug.ap[:, c, 0:257], start=True, stop=False), [totb], [PT, vaug], inc=False)
                    for m in range(2):
                        P.pe(lambda e: e.matmul(totb.ap[:, 0:257], lhsT=qp.ap[:, m, :], rhs=Cbf.ap[:, m, 0:257], start=False, stop=(m == 1)), [totb], [qp, Cbf], inc=(m == 1))
                    for m in range(2):
                        P.pe(lambda e: e.matmul(dps.ap[:, m * 512:m * 512 + 257], lhsT=kw.ap[:, c, m * 128:(m + 1) * 128], rhs=vaug.ap[:, c, 0:257], start=True, stop=True), [dps], [kw, vaug], inc=(m == 1))
                    if c + 1 < NT:
                        u1a(c + 1)
                    yield
                    P.dve(lambda e: e.tensor_tensor(out=hc.ap, in0=totb.ap[:, 0:256], in1=sigo.ap[:, c, :], op=ALU.mult), [hc], [totb, sigo])
                    P.act(lambda e: e.activation(out=ss.ap[:, 1:2], in_=totb.ap[:, 256:257], func=AF.Square), [ss], [totb])
                    P.dve(lambda e: e.bn_stats(out=st6.ap[:, 0, :], in_=hc.ap), [st6], [hc])
                    P.dve(lambda e: e.bn_aggr(out=mv.ap, in_=st6.ap[:, 0, :]), [mv], [st6])
                    P.dve(lambda e: e.tensor_scalar(out=ss.ap[:, 1:2], in0=ss.ap[:, 1:2], scalar1=1.0, scalar2=EPS, op0=ALU.max, op1=ALU.mult), [ss], [ss])
                    P.act(lambda e: e.activation(out=mv.ap[:, 1:2], in_=mv.ap[:, 1:2], func=AF.Ln, bias=ss.ap[:, 1:2], scale=1.0), [mv], [mv, ss])
                    for m in range(2):
                        P.dve(lambda e: e.scalar_tensor_tensor(out=Cst[i][h].ap[:, m, :], in0=Cst[i][h].ap[:, m, :], scalar=EB[c].ap[:, h, 127:128], in1=dps.ap[:, m * 512:m * 512 + 257], op0=ALU.mult, op1=ALU.add),
                              [Cst[i][h]], [Cst[i][h], EB[c], dps])
                    P.act(lambda e: e.activation(out=mv.ap[:, 1:2], in_=mv.ap[:, 1:2], func=AF.Exp, scale=-0.5), [mv], [mv])
                    P.dve(lambda e: e.tensor_scalar(out=hy.ap, in0=hc.ap, scalar1=mv.ap[:, 0:1], scalar2=mv.ap[:, 1:2], op0=ALU.subtract, op1=ALU.mult), [hy], [hc, mv])
                    P.dve(lambda e: e.tensor_tensor(out=hyb.ap, in0=hy.ap, in1=silz.ap[:, c, :], op=ALU.mult), [hyb], [hy, silz])
                    to_fm(hyb, lambda j: hyb.ap[:, j * 128:(j + 1) * 128], 2, Y[2],
                          lambda j0, n: Y[2].ap[:, h * 2:h * 2 + 2, c * 128:(c + 1) * 128])
                    yield

            run_threads(gen_proj_C(0))
            for h in range(8):
                run_threads(gen_core_C(h), gen_proj_C(h + 1) if h < 7 else None)

            P.mark(f"t{ti}l{l}:D")
            phase_begin()
            _rsv = CV([128, 4096], BF16, "rsv")
            ostage = [CV([128, 2048], F32, f"ostage{c}") for c in range(NT)] if last else None
            mergedT = CV([128, NB, T], BF16, "mergedT")
            gsb2 = [CV([128, T], F32, f"gsb{k_}") for k_ in range(2)]
            gtmp2 = [CV([128, T], F32, f"gtmp{k_}") for k_ in range(2)]
            gd_i = [0]
            macc = [CV([128, T], F32, f"macc{m}") for m in range(2)]
            sp_load(rowbuf[0], rowbuf[0].ap, rows_d[l * 4 + 2])
            sp_load(rowbuf[1], rowbuf[1].ap, rows_d[l * 4 + 3])
            for j8 in range(8):
                for b in range(3):
                    wg = load_w(*WIN("g", b * 8 + j8))
                    wbb = load_w("b", (l * 3 + b) * 8 + j8)
                    for m in range(2):
                        j = j8 * 2 + m
                        gd_i[0] ^= 1
                        gsb, gtmp = gsb2[gd_i[0]], gtmp2[gd_i[0]]
                        proj_fm(wg, 256, m * 128, xT,
                                lambda ps, j=j, b=b, gsb=gsb: P.act(lambda e: e.activation(out=gsb.ap, in_=ps.ap[:, 0:T], func=AF.Sigmoid, bias=cl.ap[:, b * 16 + j:b * 16 + j + 1], scale=1.0), [gsb], [ps, cl]))

                        def evac_p(ps, j=j, b=b, m=m, gsb=gsb, gtmp=gtmp):
                            if b == 0:
                                P.dve(lambda e: e.tensor_tensor(out=macc[m].ap, in0=ps.ap[:, 0:T], in1=gsb.ap, op=ALU.mult), [macc[m]], [ps, gsb])
                            else:
                                P.dve(lambda e: e.tensor_tensor(out=gtmp.ap, in0=ps.ap[:, 0:T], in1=gsb.ap, op=ALU.mult), [gtmp], [ps, gsb])
                                if b == 1:
                                    P.dve(lambda e: e.tensor_tensor(out=macc[m].ap, in0=macc[m].ap, in1=gtmp.ap, op=ALU.add), [macc[m]], [macc[m], gtmp])
                                else:
                                    P.dve(lambda e: e.tensor_tensor(out=mergedT.ap[:, j, :], in0=macc[m].ap, in1=gtmp.ap, op=ALU.add), [mergedT], [macc[m], gtmp])

                        proj_fm(wbb, 256, m * 128, Y[b], evac_p)
            for n8 in range(8):
                wo = load_w("o", l * 8 + n8)
                for c in range(NT):
                    proj_tm([wo], 256, 256, c, mergedT,
                            lambda ps, c=c, n8=n8: P.dve(lambda e: e.scalar_tensor_tensor(out=xres[c].ap[:, n8 * 256:(n8 + 1) * 256], in0=xres[c].ap[:, n8 * 256:(n8 + 1) * 256], scalar=ALPHA, in1=ps.ap[:, 0:256], op0=ALU.mult, op1=ALU.add),
                                                         [xres[c]], [xres[c], ps]))
            for c in range(NT):
                xr = xres[c]
                for s4 in range(4):
                    P.dve(lambda e, s4=s4, xr=xr: e.bn_stats(out=st6.ap[:, s4, :], in_=xr.ap[:, s4 * 512:(s4 + 1) * 512]), [st6], [xr])
                P.dve(lambda e: e.bn_aggr(out=mv.ap, in_=st6.ap.rearrange("p a b -> p (a b)")), [mv], [st6])
                rstd_from(mv.ap[:, 1:2], mv)
                P.dve(lambda e, xr=xr: e.tensor_scalar(out=xr.ap, in0=xr.ap, scalar1=mv.ap[:, 0:1], scalar2=mv.ap[:, 1:2], op0=ALU.subtract, op1=ALU.mult), [xr], [xr, mv])
                P.dve(lambda e, xr=xr: e.tensor_tensor(out=xr.ap, in0=xr.ap, in1=rowbuf[0].ap, op=ALU.mult), [xr], [xr, rowbuf[0]])
                if last:
                    og = ostage[c]
                    P.dve(lambda e, xr=xr: e.tensor_tensor(out=og.ap, in0=xr.ap, in1=rowbuf[1].ap, op=ALU.add), [og], [xr, rowbuf[1]])
                    ob = Buf(out_d[t0 + c * 128:t0 + (c + 1) * 128, :])
                    P.dma("sp", ob.ap, og.ap, [ob], [og], st_sems[c])
                    out_bufs.append(ob)
                    if st_sems[c] not in pending_out:
                        pending_out.append(st_sems[c])
                else:
                    P.dve(lambda e, xr=xr: e.tensor_tensor(out=xr.ap, in0=xr.ap, in1=rowbuf[1].ap, op=ALU.add), [xr], [xr, rowbuf[1]])

        out_bufs = []
        st_sems = [P.new_sem("store") for _ in range(NT)]
        for ti in range(NTILES):
            for i, l in enumerate(layers):
                layer_tile(ti, i, l, first=(i == 0), last=(i == NL - 1))
        P.mark("end")
        P.wait_for("sp", out_bufs)
        P.finish()
    nc._marks = P.marks
    nc._tags = P.tags
    nc._names = P.names
    return nc


def _blk(w, c0):
    return np.ascontiguousarray(w[:, c0:c0 + 256].reshape(16, 128, 256).transpose(1, 0, 2)).reshape(128, 4096)


def prep_weights(inp):
    L = DEPTH
    w_in = inp["w_in"]
    w_in_r = np.empty((L * NWBLK, 128, 4096), np.float32)
    ws_r = np.empty((L, 128, 768), np.float32)
    wb_r = np.empty((L * 24, 128, 4096), np.float32)
    wo_r = np.empty((L * 8, 128, 4096), np.float32)
    wp_r = np.empty((L * 4, 128, 2048), np.float32)
    cols_l = np.empty((L, 128, 184), np.float32)
    srow_l = np.empty((L, 128, 112), np.float32)
    rows_l = np.empty((L * 4, 128, 2048), np.float32)
    small_cols = np.concatenate([np.arange(9216, 9248), np.arange(19488, 19504)])
    for l in range(L):
        for key in SEC:
            for j in range(SECN[key]):
                w_in_r[l * NWBLK + BLK0[key] + j] = _blk(w_in[l], SEC[key] + j * 256)
        ws = w_in[l][:, small_cols]
        ws_r[l] = ws.reshape(16, 128, 48).transpose(1, 0, 2).reshape(128, 768)
        for b in range(3):
            for j8 in range(8):
                wb_r[(l * 3 + b) * 8 + j8] = _blk(inp["w_branch"][l, b], j8 * 256)
        for n8 in range(8):
            wo_r[l * 8 + n8] = _blk(inp["w_out"][l], n8 * 256)
        for g in range(4):
            wp_r[l * 4 + g] = inp["w_pool"][l, g].reshape(4, 128, 512).transpose(1, 0, 2).reshape(128, 2048)
        cols_l[l, :, 0:48] = inp["b_gate"][l].reshape(3, 16, 128).transpose(2, 0, 1).reshape(128, 48)
        cols_l[l, :, 48:64] = inp["pool_scale"][l].reshape(16, 128).T
        cols_l[l, :, 64:160] = inp["conv_w"][l].reshape(4, 24, 128).transpose(2, 0, 1).reshape(128, 96)
        cols_l[l, :, 160:184] = inp["conv_b"][l].reshape(24, 128).T
        srow = np.concatenate([inp["dt_bias"][l], inp["a_log"][l], inp["d_skip"][l], inp["i_bias"][l], inp["f_bias"][l]])
        srow_l[l] = np.broadcast_to(srow[None, :], (128, 112))
        for k, name in enumerate(["ssm_norm_w", "mlstm_norm_w", "ln_g", "ln_b"]):
            rows_l[l * 4 + k] = np.broadcast_to(inp[name][l][None, :], (128, 2048))
    r = np.arange(128)
    consts = np.zeros((128, 704), np.float32)
    consts[:, 0:128] = np.eye(128)
    consts[:, 128:256] = (r[:, None] <= r[None, :])
    consts[:, 256:384] = (r[:, None] > r[None, :])
    consts[:, 384:512] = 1.0
    consts[:, 512:640] = np.where(r[None, :] < r[:, None], -30000.0, 0.0)
    for g in range(4):
        w = 2 ** (g + 1)
        consts[:, 640 + g * 16:640 + (g + 1) * 16] = 1.0 / np.minimum(np.arange(16) + 1, w)[None, :]
    return {"w_in_r": w_in_r, "ws_r": ws_r, "wb_r": wb_r, "wo_r": wo_r, "wp_r": wp_r, "cols_l": cols_l,
            "srow_l": srow_l, "rows_l": rows_l, "consts": consts}


TILE_T = 256


def kernel(**inputs):
    inp = {k: np.asarray(v, dtype=np.float32) for k, v in inputs.items()}
    x = inp["x"]
    shared = prep_weights(inp)
    nc = build_nc(list(range(DEPTH)), TILE_T)
    in_maps = []
    for b in range(8):
        m = dict(shared)
        m["x"] = np.ascontiguousarray(x[b])
        in_maps.append(m)
    res = run_bass_kernel_spmd(nc, in_maps, core_ids=list(range(8)))
    return np.stack([np.asarray(r["out"], dtype=np.float32) for r in res.results], axis=0)
```
